# Optimizing a Trainium2 kernel written in Bass

```python
import math
import jax, jax.numpy as jnp
from jax import lax
import numpy as np

D_MODEL = 1024
BATCH = 16
SEQ = 2048
DEPTH = 1

SB_HEADS = 16
SB_HEAD_DIM = 64
SB_WIDTH = SB_HEADS * SB_HEAD_DIM
Q_BLOCK = 128
SSD_EXPAND = 2
SSD_WIDTH = SSD_EXPAND * D_MODEL
SSD_HEAD_DIM = 64
SSD_HEADS = SSD_WIDTH // SSD_HEAD_DIM
SSD_GROUPS = 4
SSD_HEADS_PER_GROUP = SSD_HEADS // SSD_GROUPS
SSD_STATE = 128
SSD_CONV = 4
SSD_CHUNK = 128
SSD_BC_WIDTH = SSD_GROUPS * SSD_STATE
SSD_CONV_DIM = SSD_WIDTH + 2 * SSD_BC_WIDTH
N_BRANCHES = 2
PROJ_SPLITS = (SB_WIDTH, SB_WIDTH, SB_WIDTH, SB_WIDTH, SSD_WIDTH, SSD_CONV_DIM, SSD_HEADS, N_BRANCHES * D_MODEL)
D_PROJ = 4 * SB_WIDTH + SSD_WIDTH + SSD_CONV_DIM + SSD_HEADS + N_BRANCHES * D_MODEL
EPS = 1e-6
DT_MIN = 0.001
DT_MAX = 0.1
A_INIT_MIN = 1.0
A_INIT_MAX = 16.0

kernel_name = "hybrid_stickbreaking_ssd_gated_block"


def rms_norm(x, w):
    xf = x.astype(jnp.float32)
    y = xf * lax.rsqrt(jnp.mean(xf * xf, axis=-1, keepdims=True) + EPS)
    return (y * w.astype(jnp.float32)).astype(x.dtype)


def stick_breaking_attention(q, k, v):
    s_len = q.shape[2]
    scale = q.shape[-1] ** -0.5
    outs = []
    for blk in range(s_len // Q_BLOCK):
        start = blk * Q_BLOCK
        end = start + Q_BLOCK
        qb = q[:, :, start:end]
        kb = k[:, :, :end]
        vb = v[:, :, :end]
        z = jnp.einsum('bhqd,bhkd->bhqk', qb, kb) * scale
        t_idx = start + jnp.arange(Q_BLOCK)[:, None]
        s_idx = jnp.arange(end)[None, :]
        mask = s_idx < t_idx
        log_beta = jax.nn.log_sigmoid(z)
        log_one_minus = jnp.where(mask, jax.nn.log_sigmoid(-z), 0.0)
        later = lax.cumsum(log_one_minus, axis=3, reverse=True) - log_one_minus
        a = jnp.where(mask, jnp.exp(log_beta + later), 0.0)
        outs.append(jnp.einsum('bhqk,bhkd->bhqd', a, vb))
    return jnp.concatenate(outs, axis=2)


def causal_depthwise_conv(x, w, b):
    c = x.shape[-1]
    y = lax.conv_general_dilated(
        x, w[:, None, :].astype(x.dtype), window_strides=(1,),
        padding=[(SSD_CONV - 1, 0)],
        dimension_numbers=('NWC', 'WIO', 'NWC'),
        feature_group_count=c)
    return y + b.astype(x.dtype)


def ssd_chunked(x, dt, a, bm, cm):
    b, s, g, hg, p = x.shape
    n = bm.shape[-1]
    nc = s // SSD_CHUNK
    x = x.reshape(b, nc, SSD_CHUNK, g, hg, p)
    dt = dt.reshape(b, nc, SSD_CHUNK, g, hg)
    bm = bm.reshape(b, nc, SSD_CHUNK, g, n)
    cm = cm.reshape(b, nc, SSD_CHUNK, g, n)
    a_cs = jnp.cumsum(dt * a, axis=2)
    xdt = x * dt[..., None]
    l_idx = jnp.arange(SSD_CHUNK)
    causal = (l_idx[:, None] >= l_idx[None, :])[:, :, None, None]
    seg = a_cs[:, :, :, None] - a_cs[:, :, None, :]
    decay = jnp.exp(jnp.where(causal, seg, -jnp.inf))
    cb = jnp.einsum('bclgn,bcsgn->bclsg', cm, bm)
    y_diag = jnp.einsum('bclsgh,bcsghp->bclghp', cb[..., None] * decay, xdt)
    decay_to_end = jnp.exp(a_cs[:, :, -1:] - a_cs)
    states = jnp.einsum('bclgn,bclghp->bcghpn', bm, xdt * decay_to_end[..., None])
    chunk_decay = jnp.exp(a_cs[:, :, -1])

    def step(h_prev, inp):
        st, dec = inp
        return h_prev * dec[..., None, None] + st, h_prev

    h0 = jnp.zeros((b, g, hg, p, n), jnp.float32)
    _, h_in = lax.scan(step, h0, (jnp.moveaxis(states, 1, 0), jnp.moveaxis(chunk_decay, 1, 0)))
    h_in = jnp.moveaxis(h_in, 0, 1)
    y_off = jnp.einsum('bclgn,bcghpn->bclghp', cm, h_in) * jnp.exp(a_cs)[..., None]
    return (y_diag + y_off).reshape(b, s, g, hg, p)


def hybrid_layer(x, norm_w, w_in, conv_w, conv_b, dt_bias, a_log, d_skip,
                 ssm_norm_w, w_attn_out, w_ssm_out, w_o):
    b, s, _ = x.shape
    f32 = jnp.float32
    h = rms_norm(x, norm_w)
    proj = jnp.einsum('bsd,de->bse', h, w_in)
    split_points = [int(v) for v in np.cumsum(PROJ_SPLITS)[:-1]]
    q, k, v, z_a, z_s, xbc, dt_raw, gate_raw = jnp.split(proj, split_points, axis=-1)

    def to_heads(t):
        return t.reshape(b, s, SB_HEADS, SB_HEAD_DIM).transpose(0, 2, 1, 3).astype(f32)

    o = stick_breaking_attention(to_heads(q), to_heads(k), to_heads(v))
    o = o.transpose(0, 2, 1, 3).reshape(b, s, SB_WIDTH)
    y_a = (o * jax.nn.silu(z_a.astype(f32))).astype(x.dtype)
    y_a = jnp.einsum('bse,ed->bsd', y_a, w_attn_out)

    xbc = jax.nn.silu(causal_depthwise_conv(xbc, conv_w, conv_b))
    xs, bm, cm = jnp.split(xbc, [SSD_WIDTH, SSD_WIDTH + SSD_BC_WIDTH], axis=-1)
    xs = xs.reshape(b, s, SSD_GROUPS, SSD_HEADS_PER_GROUP, SSD_HEAD_DIM).astype(f32)
    bm = bm.reshape(b, s, SSD_GROUPS, SSD_STATE).astype(f32)
    cm = cm.reshape(b, s, SSD_GROUPS, SSD_STATE).astype(f32)
    dt = jax.nn.softplus(dt_raw.astype(f32) + dt_bias.astype(f32))
    dt = dt.reshape(b, s, SSD_GROUPS, SSD_HEADS_PER_GROUP)
    a = -jnp.exp(a_log.astype(f32)).reshape(SSD_GROUPS, SSD_HEADS_PER_GROUP)
    y = ssd_chunked(xs, dt, a, bm, cm)
    y = y + xs * d_skip.astype(f32).reshape(SSD_GROUPS, SSD_HEADS_PER_GROUP)[..., None]
    y = y.reshape(b, s, SSD_WIDTH) * jax.nn.silu(z_s.astype(f32))
    yg = y.reshape(b, s, SSD_GROUPS, SSD_WIDTH // SSD_GROUPS)
    yg = yg * lax.rsqrt(jnp.mean(yg * yg, axis=-1, keepdims=True) + EPS)
    y = yg.reshape(b, s, SSD_WIDTH) * ssm_norm_w.astype(f32)
    y_s = jnp.einsum('bse,ed->bsd', y.astype(x.dtype), w_ssm_out)

    g_a, g_s = jnp.split(jax.nn.sigmoid(gate_raw.astype(f32)), N_BRANCHES, axis=-1)
    merged = (g_a * y_a.astype(f32) + g_s * y_s.astype(f32)).astype(x.dtype)
    return x + jnp.einsum('bsd,de->bse', merged, w_o)


def setup_inputs(seed: int = 0) -> dict:
    key = jax.random.key(seed)
    ks = jax.random.split(key, 14)
    f32 = jnp.float32
    x = jax.random.normal(ks[0], (BATCH, SEQ, D_MODEL), f32)
    norm_w = 1.0 + 0.02 * jax.random.normal(ks[1], (DEPTH, D_MODEL), f32)
    w_in = jax.random.normal(ks[2], (DEPTH, D_MODEL, D_PROJ), f32) * D_MODEL ** -0.5
    conv_w = jax.random.normal(ks[3], (DEPTH, SSD_CONV, SSD_CONV_DIM), f32) * SSD_CONV ** -0.5
    conv_b = 0.02 * jax.random.normal(ks[4], (DEPTH, SSD_CONV_DIM), f32)
    u = jax.random.uniform(ks[5], (DEPTH, SSD_HEADS), f32)
    dt0 = jnp.exp(u * (math.log(DT_MAX) - math.log(DT_MIN)) + math.log(DT_MIN))
    dt_bias = dt0 + jnp.log(-jnp.expm1(-dt0))
    a_log = jnp.log(jax.random.uniform(ks[6], (DEPTH, SSD_HEADS), f32, A_INIT_MIN, A_INIT_MAX))
    d_skip = 1.0 + 0.02 * jax.random.normal(ks[7], (DEPTH, SSD_HEADS), f32)
    ssm_norm_w = 1.0 + 0.02 * jax.random.normal(ks[8], (DEPTH, SSD_WIDTH), f32)
    w_attn_out = jax.random.normal(ks[9], (DEPTH, SB_WIDTH, D_MODEL), f32) * SB_WIDTH ** -0.5
    w_ssm_out = jax.random.normal(ks[10], (DEPTH, SSD_WIDTH, D_MODEL), f32) * SSD_WIDTH ** -0.5
    w_o = jax.random.normal(ks[11], (DEPTH, D_MODEL, D_MODEL), f32) * D_MODEL ** -0.5
    final_norm_w = 1.0 + 0.02 * jax.random.normal(ks[12], (D_MODEL,), f32)
    return {"x": x, "norm_w": norm_w, "w_in": w_in, "conv_w": conv_w, "conv_b": conv_b,
            "dt_bias": dt_bias, "a_log": a_log, "d_skip": d_skip, "ssm_norm_w": ssm_norm_w,
            "w_attn_out": w_attn_out, "w_ssm_out": w_ssm_out, "w_o": w_o,
            "final_norm_w": final_norm_w}


def reference(x, norm_w, w_in, conv_w, conv_b, dt_bias, a_log, d_skip, ssm_norm_w,
              w_attn_out, w_ssm_out, w_o, final_norm_w):
    h = x
    for layer in range(DEPTH):
        h = hybrid_layer(h, norm_w[layer], w_in[layer], conv_w[layer], conv_b[layer],
                         dt_bias[layer], a_log[layer], d_skip[layer], ssm_norm_w[layer],
                         w_attn_out[layer], w_ssm_out[layer], w_o[layer])
    return rms_norm(h, final_norm_w)
```

```python
import numpy as np
import concourse.bass as bass
import concourse.mybir as mybir

F32 = mybir.dt.float32
BF16 = mybir.dt.bfloat16
AF = mybir.ActivationFunctionType
ALU = mybir.AluOpType
AX = mybir.AxisListType

PE, ACT, DVE, POOL, SP = "pe", "act", "dve", "pool", "sp"
COMPUTE = (PE, ACT, DVE, POOL)
NDMASEM = 8
STRICT_SAME_ENGINE = [True]


def _region(ap):
    t = ap.tensor
    name = t.name
    pat = ap.ap
    off = int(ap.offset)
    es = mybir.dt.size(ap.dtype)
    space = str(ap.space)
    if "DRAM" in space.upper() or "HBM" in space.upper():
        ext = 0
        for (s, c) in pat:
            ext += (c - 1) * abs(s)
        return (name, 0, 1, off * es, (off + ext + 1) * es)
    prow = pat[0][0]
    pcnt = pat[0][1]
    if prow == 0:
        prow = 1 << 40
    p0 = off // prow
    f0 = off % prow
    ext = 0
    for (s, c) in pat[1:]:
        ext += (c - 1) * abs(s)
    if "PSUM" in space.upper():
        b0 = (f0 * es) // 2048 * 2048
        b1 = ((f0 + ext + 1) * es + 2047) // 2048 * 2048
        return (name, 0, 128, b0, b1)
    return (name, p0, p0 + pcnt, f0 * es, (f0 + ext + 1) * es)


class Op:
    __slots__ = ("eng", "fn", "idx", "deps", "sig", "signum", "is_dma", "dslot", "dval", "tag")

    def __init__(self, eng, fn, idx, is_dma=False, tag=""):
        self.eng = eng
        self.fn = fn
        self.idx = idx
        self.deps = {}
        self.sig = False
        self.signum = 0
        self.is_dma = is_dma
        self.dslot = None
        self.dval = 0
        self.tag = tag


class Prog:
    def __init__(self, nc):
        self.nc = nc
        self.ops = []
        self.track = {}
        self.untracked = set()
        self.dma_count = {}
        self.dma_rr = {SP: 0, POOL: 0, ACT: 0}

    def _key(self, op):
        if op.is_dma:
            return ("dma", op.eng, op.dslot)
        return op.eng

    def _add_dep(self, op, other_idx):
        o = self.ops[other_idx]
        k = self._key(o)
        if (not o.is_dma) and (not op.is_dma) and o.eng == op.eng and op.eng == PE:
            return
        cur = op.deps.get(k)
        if cur is None or cur < other_idx:
            op.deps[k] = other_idx

    def _access(self, op, ap, is_write, register=True):
        reg = _region(ap)
        name = reg[0]
        if name in self.untracked:
            return
        box = reg[1:]
        is_psum = (box[0] == 0 and box[1] == 128 and name.startswith("pbig"))
        ent = self.track.setdefault(name, {})
        p0, p1, f0, f1 = box
        dead = []
        for (b, k, w), idx in ent.items():
            if idx == op.idx:
                continue
            if b[0] < p1 and p0 < b[1] and b[2] < f1 and f0 < b[3]:
                if is_write or w or (is_psum and k != self._key(op)):
                    o = self.ops[idx]
                    same = (not o.is_dma) and (not op.is_dma) and o.eng == op.eng
                    if same and not STRICT_SAME_ENGINE[0] and not (w and not is_write):
                        pass
                    else:
                        self._add_dep(op, idx)
                if is_write and b[0] >= p0 and b[1] <= p1 and b[2] >= f0 and b[3] <= f1:
                    dead.append((b, k, w))
        if not register:
            return
        for d in dead:
            del ent[d]
        ent[(box, self._key(op), is_write)] = op.idx

    def add(self, eng, fn, reads=(), writes=(), is_dma=False, tag=""):
        op = Op(eng, fn, len(self.ops), is_dma=is_dma, tag=tag)
        if is_dma:
            slot = self.dma_rr[eng] % NDMASEM
            self.dma_rr[eng] += 1
            op.dslot = slot
            n = self.dma_count.get((eng, slot), 0) + 1
            self.dma_count[(eng, slot)] = n
            op.dval = 16 * n
        self.ops.append(op)
        reg = fn is not None
        for r in reads:
            if r is not None and not isinstance(r, (int, float)):
                self._access(op, r, False, reg)
        for w in writes:
            if w is not None:
                self._access(op, w, True, reg)
        for k, idx in op.deps.items():
            self.ops[idx].sig = True
        return op

    def mm(self, out, lhsT, rhs, start=True, stop=True, **kw):
        if kw.pop("sgc", False):
            kw["skip_group_check"] = True
        return self.add(PE, lambda e: e.matmul(out, lhsT, rhs, start=start, stop=stop, **kw),
                        reads=[lhsT, rhs], writes=[out], tag="mm")

    def tr(self, out, in_, ident):
        return self.add(PE, lambda e: e.transpose(out, in_, ident), reads=[in_, ident], writes=[out], tag="tr")

    def act(self, out, in_, func, bias=None, scale=None, accum_out=None, eng=ACT):
        kw = {}
        if bias is not None:
            kw["bias"] = bias
        if scale is not None:
            kw["scale"] = scale
        if accum_out is not None:
            kw["accum_out"] = accum_out
        return self.add(ACT, lambda e: e.activation(out, in_, func, **kw),
                        reads=[in_, bias, scale], writes=[out, accum_out], tag="act")

    def tt(self, eng, out, in0, in1, op):
        return self.add(eng, lambda e: e.tensor_tensor(out, in0, in1, op), reads=[in0, in1], writes=[out], tag="tt")

    def ts(self, eng, out, in0, s1, s2, op0, op1=None, accum_out=None):
        kw = {}
        if op1 is not None:
            kw["op1"] = op1
        if accum_out is not None:
            kw["accum_out"] = accum_out
        return self.add(eng, lambda e: e.tensor_scalar(out, in0, s1, s2, op0, **kw),
                        reads=[in0, s1, s2], writes=[out, accum_out], tag="ts")

    def stt(self, out, in0, scalar, in1, op0, op1, accum_out=None, eng=DVE):
        kw = {}
        if accum_out is not None:
            kw["accum_out"] = accum_out
        return self.add(eng, lambda e: e.scalar_tensor_tensor(out, in0, scalar, in1, op0, op1, **kw),
                        reads=[in0, scalar, in1], writes=[out, accum_out], tag="stt")

    def copy(self, eng, out, in_):
        if eng == ACT:
            return self.add(ACT, lambda e: e.copy(out, in_), reads=[in_], writes=[out], tag="copy")
        return self.add(eng, lambda e: e.tensor_copy(out, in_), reads=[in_], writes=[out], tag="copy")

    def memset(self, eng, ap, val):
        return self.add(eng, lambda e: e.memset(ap, val), writes=[ap], tag="memset")

    def dma(self, queue, out, in_, **kw):
        return self.add(queue, lambda e: e.dma_start(out, in_, **kw), reads=[in_], writes=[out], is_dma=True, tag="dma")

    def fence(self, eng, reads=(), writes=()):
        return self.add(eng, None, reads=reads, writes=writes, tag="fence")

    def emit(self):
        nc = self.nc
        cnt = {e: 0 for e in COMPUTE + (SP,)}
        for op in self.ops:
            if op.is_dma or op.fn is None:
                continue
            if op.sig:
                cnt[op.eng] += 1
                op.signum = cnt[op.eng]
        import contextlib
        with contextlib.ExitStack() as st:
            sems = {e: st.enter_context(nc.semaphore("s_" + e)) for e in COMPUTE}
            dsems = {}
            for q in (SP, POOL, ACT):
                for s in range(NDMASEM):
                    if (q, s) in self.dma_count:
                        dsems[(q, s)] = st.enter_context(nc.semaphore("d_%s%d" % (q, s)))
            block = st.enter_context(nc.Block())
            per_eng = {e: [] for e in COMPUTE + (SP,)}
            for op in self.ops:
                per_eng[op.eng].append(op)

            def run(engname, eng):
                waited = {}
                for op in per_eng[engname]:
                    for k, idx in op.deps.items():
                        o = self.ops[idx]
                        if o.is_dma:
                            sem = dsems[(o.eng, o.dslot)]
                            val = o.dval
                        else:
                            if o.fn is None:
                                continue
                            sem = sems[o.eng]
                            val = o.signum
                        if waited.get(k, 0) >= val:
                            continue
                        waited[k] = val
                        eng.wait_ge(sem, val)
                    if op.fn is None:
                        continue
                    if op.is_dma and op.dval > 16:
                        kk = ("dma", op.eng, op.dslot)
                        if waited.get(kk, 0) < op.dval - 16:
                            waited[kk] = op.dval - 16
                            eng.wait_ge(dsems[(op.eng, op.dslot)], op.dval - 16)
                    ins = op.fn(eng)
                    if op.is_dma:
                        ins.then_inc(dsems[(op.eng, op.dslot)], 16)
                    elif op.sig:
                        ins.then_inc(sems[op.eng], 1)

            @block.tensor
            def _(e):
                run(PE, e)

            @block.scalar
            def _(e):
                run(ACT, e)

            @block.vector
            def _(e):
                run(DVE, e)

            @block.gpsimd
            def _(e):
                run(POOL, e)

            @block.sync
            def _(e):
                run(SP, e)


import contextlib
from concourse.bass_utils import run_bass_kernel_spmd

S = 2048
DM = 1024
NSEQ = 2
EPS = 1e-6
CH = [(i * 128, 128) for i in range(72)] + [(9216, 32)] + [(9248 + i * 128, 128) for i in range(16)]
C_Q, C_K, C_V, C_ZA, C_ZS, C_XS, C_B, C_C, C_DT, C_GA, C_GS = 0, 8, 16, 24, 32, 48, 64, 68, 72, 73, 81
NCH = len(CH)

DEBUG = {}
STOP = [None]
SKIP = set()
FILL_N = [0]


class StopBuild(Exception):
    pass


def stop_at(tag):
    if STOP[0] == tag:
        raise StopBuild()


def build_program(nseq=NSEQ, debug=False):
    nc = bass.Bass("TRN2", target_bir_lowering=False)
    NT = nseq * S
    x = nc.dram_tensor("x", [NT, DM], F32, kind="ExternalInput").ap()
    norm_w = nc.dram_tensor("norm_w", [1, DM], F32, kind="ExternalInput").ap()
    w_in = nc.dram_tensor("w_in", [DM, 11296], F32, kind="ExternalInput").ap()
    conv_w = nc.dram_tensor("conv_w", [4, 3072], F32, kind="ExternalInput").ap()
    conv_b = nc.dram_tensor("conv_b", [1, 3072], F32, kind="ExternalInput").ap()
    dt_bias = nc.dram_tensor("dt_bias", [1, 32], F32, kind="ExternalInput").ap()
    a_log = nc.dram_tensor("a_log", [1, 32], F32, kind="ExternalInput").ap()
    d_skip = nc.dram_tensor("d_skip", [1, 32], F32, kind="ExternalInput").ap()
    ssm_norm_w = nc.dram_tensor("ssm_norm_w", [1, 2048], F32, kind="ExternalInput").ap()
    w_attn_out = nc.dram_tensor("w_attn_out", [1024, 1024], F32, kind="ExternalInput").ap()
    w_ssm_out = nc.dram_tensor("w_ssm_out", [2048, 1024], F32, kind="ExternalInput").ap()
    w_o = nc.dram_tensor("w_o", [1024, 1024], F32, kind="ExternalInput").ap()
    final_norm_w = nc.dram_tensor("final_norm_w", [1, DM], F32, kind="ExternalInput").ap()
    y = nc.dram_tensor("y", [NT, DM], F32, kind="ExternalOutput").ap()
    ws_in = nc.dram_tensor("ws_in", [NCH, 128, 8, 128], BF16, kind="Internal").ap()
    ws_ao = nc.dram_tensor("ws_ao", [8, 128, 8, 128], BF16, kind="Internal").ap()
    ws_so = nc.dram_tensor("ws_so", [8, 128, 16, 128], BF16, kind="Internal").ap()
    ws_o = nc.dram_tensor("ws_o", [2, 128, 8, 512], BF16, kind="Internal").ap()
    dbg = {}
    if debug:
        for nm, shp, dt_ in debug:
            dbg[nm] = nc.dram_tensor(nm, shp, dt_, kind="ExternalOutput").ap()

    with contextlib.ExitStack() as st:
        def sb(n, s, d):
            return st.enter_context(nc.sbuf_tensor(n, s, d))
        P = Prog(nc)
        for nm in ("x", "norm_w", "w_in", "conv_w", "conv_b", "dt_bias", "a_log", "d_skip", "ssm_norm_w",
                   "w_attn_out", "w_ssm_out", "w_o", "final_norm_w"):
            P.untracked.add(nm)

        hT = sb("hT", [128, 8, S], BF16)
        OG = sb("OG", [128, 8, S], BF16)
        ident = sb("ident", [128, 128], BF16)
        mT = sb("mT", [128, 128], BF16)
        mLo = sb("mLo", [128, 128], BF16)
        mLE = sb("mLE", [128, 128], BF16)
        mTS = sb("mTS", [128, 128], BF16)
        ones = sb("ones", [128, 128], BF16)
        mNeg = sb("mNeg", [128, 128], BF16)
        zeros = sb("zeros", [128, 128], BF16)
        dtb_bc = sb("dtb_bc", [128, 32], F32)
        a_bc = sb("a_bc", [128, 32], F32)
        dsk_bc = sb("dsk_bc", [128, 32], F32)
        convw = sb("convw", [128, 24, 4], F32)
        convb = sb("convb", [128, 24], F32)
        nhalf = sb("nhalf", [128, 1], F32)
        snw_col = sb("snw_col", [128, 16], F32)
        Hin32 = sb("Hin32", [128, 4, 512], F32)
        Hin16 = sb("Hin16", [128, 4, 512], BF16)
        carry = sb("carry", [128, 24, 4], BF16)
        small = sb("small", [128, 16], F32)
        arena_t = sb("arena", [128, 64 * 1024], BF16)
        pbig = st.enter_context(nc.psum_tensor("pbig", [128, 4096], F32))

        def bank(i, n=1):
            return pbig[:, i * 512:(i + n) * 512]

        class Arena:
            def __init__(self):
                self.off = 0

            def alloc(self, shape, dtype):
                n = 1
                for v in shape:
                    n *= v
                nb = n * (4 if dtype == F32 else 2)
                ne = (nb // 2 + 15) // 16 * 16
                v = arena_t[:, self.off:self.off + nb // 2]
                self.off += ne
                assert self.off <= 64 * 1024, self.off
                if dtype == F32:
                    v = v.bitcast(F32)
                if len(shape) == 2:
                    v = v.rearrange("p (a b) -> p a b", a=shape[0])
                elif len(shape) == 3:
                    v = v.rearrange("p (a b c) -> p a b c", a=shape[0], b=shape[1])
                return v

        def mask(t, op, sgn=1, base=0):
            P.memset(POOL, t[:], 1.0)
            P.add(POOL, lambda e: e.affine_select(t[:], t[:], [[-sgn, 128]], op, 0.0, base=base, channel_multiplier=sgn),
                  reads=[t[:]], writes=[t[:]])
        mask(ident, ALU.is_equal)
        mask(mT, ALU.is_ge, 1, 0)
        mask(mLo, ALU.is_ge, -1, -1)
        mask(mLE, ALU.is_ge, -1, 0)
        mask(mTS, ALU.is_ge, 1, -1)
        P.memset(POOL, ones[:], 1.0)
        P.memset(POOL, mNeg[:], -30000.0)
        P.add(POOL, lambda e: e.affine_select(mNeg[:], mNeg[:], [[-1, 128]], ALU.is_ge, 0.0, base=0, channel_multiplier=1),
              reads=[mNeg[:]], writes=[mNeg[:]])
        P.memset(POOL, zeros[:], 0.0)
        P.memset(POOL, nhalf[:], -0.5)
        P.memset(POOL, Hin32[:], 0.0)
        P.memset(POOL, Hin16[:], 0.0)
        P.memset(POOL, carry[:], 0.0)
        def late_consts():
            P.dma(SP, dtb_bc[:], dt_bias.broadcast_to([128, 32]))
            P.dma(SP, a_bc[:], a_log.broadcast_to([128, 32]))
            P.dma(SP, dsk_bc[:], d_skip.broadcast_to([128, 32]))
            cwv = conv_w.rearrange("k (c p) -> k c p", p=128)
            cbv = conv_b.rearrange("o (c p) -> o c p", p=128)
            for c in range(24):
                for k in range(4):
                    P.dma(SP, convw[:, c, k:k + 1], cwv[k, c, :].unsqueeze(1))
                P.dma(SP, convb[:, c:c + 1], cbv[0, c, :].unsqueeze(1))
            snv = ssm_norm_w.rearrange("o (c p) -> o c p", p=128)
            for c in range(16):
                P.dma(SP, snw_col[:, c:c + 1], snv[0, c, :].unsqueeze(1))
            P.act(a_bc[:], a_bc[:], AF.Exp)
            P.ts(DVE, a_bc[:], a_bc[:], -1.0, None, ALU.mult)

        w_in_v = w_in.rearrange("(k p) n -> p k n", p=128)

        def conv_chunks(c0, n):
            col0 = CH[c0][0]
            wd = CH[c0][1]
            if wd == 128:
                for i in range(n):
                    P.dma(POOL, ws_in[c0 + i], w_in_v[:, :, col0 + i * 128:col0 + (i + 1) * 128])
            else:
                P.dma(POOL, ws_in[c0][:, :, 0:wd], w_in_v[:, :, col0:col0 + wd])
        for hp in range(8):
            for base in (C_Q, C_K, C_V, C_ZA):
                conv_chunks(base + hp, 1)
        for c0 in range(C_ZS, C_DT, 2):
            conv_chunks(c0, 2)
        conv_chunks(C_DT, 1)
        for c0 in range(C_GA, NCH, 2):
            conv_chunks(c0, 2)
        wao_v = w_attn_out.rearrange("(k p) n -> p k n", p=128)
        wso_v = w_ssm_out.rearrange("(k p) n -> p k n", p=128)
        wo_v = w_o.rearrange("(k p) n -> p k n", p=128)
        for dc in range(8):
            P.dma(POOL, ws_ao[dc], wao_v[:, :, dc * 128:(dc + 1) * 128])
        for dc in range(8):
            P.dma(POOL, ws_so[dc], wso_v[:, :, dc * 128:(dc + 1) * 128])
        for nh in range(2):
            P.dma(POOL, ws_o[nh], wo_v[:, :, nh * 512:(nh + 1) * 512])

        def dbg_out(name, src_ap, dst_slice=None):
            if name in dbg:
                d = dbg[name] if dst_slice is None else dst_slice(dbg[name])
                P.dma(SP, d, src_ap)

        trb = bank(7).bitcast(BF16)

        def main_body():
          stop_at("init")
          for s in range(nseq):
              row0 = s * S
              ar = Arena()
              junk = ar.alloc((DM,), BF16)
              hb = [ar.alloc((DM,), BF16) for _ in range(2)]
              normw_bc = ar.alloc((DM,), F32)
              xt = [ar.alloc((DM,), F32) for _ in range(2)]
              P.dma(SP, normw_bc, norm_w.broadcast_to([128, DM]))
              for i in range(16):
                  xi = xt[i % 2]
                  so = (i % 2) * 12
                  trb_ = bank(7 - (i % 2)).bitcast(BF16)
                  P.dma(SP, xi, x[row0 + i * 128: row0 + (i + 1) * 128, :])
                  ss = small[:, so:so + 1]
                  P.act(junk, xi, AF.Square, accum_out=ss)
                  P.ts(DVE, small[:, so + 1:so + 2], ss, 1.0 / DM, EPS, ALU.mult, ALU.add)
                  P.act(small[:, so + 3:so + 4], small[:, so + 1:so + 2], AF.Ln)
                  P.act(small[:, so + 2:so + 3], small[:, so + 3:so + 4], AF.Exp, scale=-0.5)
                  h_ = hb[i % 2]
                  P.stt(h_, xi, small[:, so + 2:so + 3], normw_bc, ALU.mult, ALU.mult)
                  for k in range(8):
                      P.tr(trb_[:, k * 128:(k + 1) * 128], h_[:, k * 128:(k + 1) * 128], ident[:])
                  P.copy(ACT, hT[:, :, i * 128:(i + 1) * 128], trb_.rearrange("p (k t) -> p k t", k=8))
              if s == 0:
                  dbg_out("d_hT", hT[:])
              stop_at("A")

              ar = Arena()
              qT = [ar.alloc((S,), BF16) for _ in range(2)]
              ksp = [[ar.alloc((S,), BF16) for _ in range(2)] for _ in range(2)]
              sz = [ar.alloc((S,), BF16) for _ in range(2)]
              Vp = [[ar.alloc((16, 128), BF16) for _ in range(2)] for _ in range(2)]
              Wb = [ar.alloc((4, 8, 128), BF16) for _ in range(2)]
              e32 = [[ar.alloc((512,), F32) for _ in range(2)] for _ in range(2)]
              spb = [[ar.alloc((512,), BF16) for _ in range(2)] for _ in range(2)]
              Pm = [[ar.alloc((512,), F32) for _ in range(2)] for _ in range(2)]
              ab = [[ar.alloc((512,), BF16) for _ in range(2)] for _ in range(2)]
              for par_ in range(2):
                  P.memset(DVE, ksp[par_][0][64:128, :], 0.0)
                  P.memset(DVE, ksp[par_][1][0:64, :], 0.0)
                  P.memset(DVE, Vp[par_][0][:, :, 64:128], 0.0)
                  P.memset(DVE, Vp[par_][1][:, :, 0:64], 0.0)
              zb = [bank(0), bank(1)]
              acc = [bank(2), bank(3)]
              ob = [bank(4), bank(5)]
              pj = [bank(6), bank(7)]
              pjc = [0]

              def load_w(hp, par):
                  for gi, base in enumerate((C_Q, C_K, C_V, C_ZA)):
                      P.dma(SP, Wb[par][:, gi], ws_in[base + hp])

              def proj_groups(hp, par):
                  micro = []
                  W = Wb[par]

                  def fm(gi, tt, kind):
                      pbi = pjc[0] % 2
                      pjc[0] += 1
                      pb = pj[pbi]
                      for k in range(8):
                          micro.append(lambda k=k: P.mm(pb, W[:, gi, k, :], hT[:, k, tt * 512:(tt + 1) * 512],
                                                        start=(k == 0), stop=(k == 7)))
                      sl = slice(tt * 512, (tt + 1) * 512)

                      def ev():
                          if kind == "q":
                              P.copy(DVE, qT[par][:, sl], pb)
                          elif kind == "k":
                              P.ts(DVE, ksp[par][0][0:64, sl], pb[0:64, :], 0.125, None, ALU.mult)
                              P.ts(DVE, ksp[par][1][64:128, sl], pb[64:128, :], 0.125, None, ALU.mult)
                          else:
                              P.act(sz[par][:, sl], pb, AF.Silu)
                      micro.append(ev)

                  def vg(vb):
                      pbi = pjc[0] % 2
                      pjc[0] += 1
                      pb = pj[pbi]
                      for j in range(4):
                          tb = vb * 4 + j
                          for k in range(0, 8, 2):
                              def mm2(j=j, tb=tb, k=k):
                                  for kk in (k, k + 1):
                                      P.mm(pb[:, j * 128:(j + 1) * 128], hT[:, kk, tb * 128:(tb + 1) * 128], W[:, 2, kk, :],
                                           start=(kk == 0), stop=(kk == 7))
                              micro.append(mm2)

                      def ev():
                          pv = pb.rearrange("p (j n) -> p j n", j=4)
                          P.copy(DVE, Vp[par][0][:, vb * 4:(vb + 1) * 4, 0:64], pv[:, :, 0:64])
                          P.copy(DVE, Vp[par][1][:, vb * 4:(vb + 1) * 4, 64:128], pv[:, :, 64:128])
                      micro.append(ev)
                  for tt in range(4):
                      fm(0, tt, "q")
                      fm(1, tt, "k")
                      vg(tt)
                  for tt in range(4):
                      fm(3, tt, "za")
                  return micro

              def attention_pair(hp, par, pending):
                  steps = [(qc, sbk) for qc in range(4) for sbk in range(4 * qc + 3, -1, -1)]

                  def geo(qc, sbk):
                      c0 = (sbk - 4 * qc) * 128 if sbk >= 4 * qc else 0
                      return c0, (sbk >= 4 * qc)

                  def emit_z(th, i):
                      qc, sbk = steps[i]
                      c0, dg_ = geo(qc, sbk)
                      P.mm(zb[th][:, c0:512], ksp[par][th][:, sbk * 128:(sbk + 1) * 128],
                           qT[par][:, qc * 512 + c0:(qc + 1) * 512], start=True, stop=True)
                      if dg_:
                          P.mm(zb[th][:, c0:c0 + 128], ident[:], mNeg[:], start=False, stop=True, sgc=True)
                  n = len(steps)

                  def fill():
                      for _ in range(FILL_N[0]):
                          P.mm(ob[1], ident[:], qT[par][:, 0:512], start=True, stop=True)

                  def S1(th, i):
                      qc, sbk = steps[i]
                      c0, diag = geo(qc, sbk)
                      bi = i % 2
                      if sbk == 4 * qc + 3:
                          P.mm(acc[th], zeros[:], qT[par][:, 0:512], start=True, stop=True)
                      e_, s_ = e32[th][bi], spb[th][bi]
                      P.act(e_[:, c0:512], zb[th][:, c0:512], AF.Exp)
                      P.act(s_[:, c0:512], e_[:, c0:512], AF.Ln, bias=1.0)
                      fill()
                      P.mm(acc[th][:, c0:512], mT[:], s_[:, c0:512], start=False, stop=True, sgc=True)
                      if i + 1 < n:
                          emit_z(th, i + 1)

                  def S2(th, i):
                      qc, sbk = steps[i]
                      c0, diag = geo(qc, sbk)
                      bi = i % 2
                      obk = ob[qc % 2] if FILL_N[0] == 0 else ob[0]
                      e_, s_, p_, a_ = e32[th][bi], spb[th][bi], Pm[th][bi], ab[th][bi]
                      if th == 0 and sbk == 4 * qc + 3:
                          P.mm(obk, zeros[:], qT[par][:, 0:512], start=True, stop=True)
                      P.act(p_[:, c0:512], acc[th][:, c0:512], AF.Exp, scale=-1.0)
                      P.tt(DVE, a_[:, c0:512], p_[:, c0:512], e_[:, c0:512], ALU.mult)
                      fill()
                      if sbk != 0:
                          P.mm(acc[th][:, c0:512], mLo[:], s_[:, c0:512], start=False, stop=True, sgc=True)
                      P.mm(obk[:, c0:512], Vp[par][th][:, sbk, :], a_[:, c0:512], start=False, stop=True, sgc=True)
                      if th == 1 and sbk == 0:
                          P.tt(DVE, OG[:, hp, qc * 512:(qc + 1) * 512], obk, sz[par][:, qc * 512:(qc + 1) * 512], ALU.mult)

                  nmicro = len(pending)
                  per = (nmicro + 4 * n - 1) // (4 * n) if nmicro else 0

                  def pop():
                      for _ in range(per):
                          if pending:
                              pending.pop(0)()
                  for th in range(2):
                      emit_z(th, 0)
                  S1(0, 0)
                  for i in range(n):
                      S1(1, i)
                      pop()
                      S2(0, i)
                      pop()
                      if i + 1 < n:
                          S1(0, i + 1)
                      pop()
                      S2(1, i)
                      pop()
                  while pending:
                      pending.pop(0)()

              if "B" not in SKIP:
                  load_w(0, 0)
                  for g in proj_groups(0, 0):
                      g()
              for hp in range(8 if "B" not in SKIP else 0):
                  par = hp % 2
                  pending = []
                  if hp + 1 < 8:
                      load_w(hp + 1, 1 - par)
                      pending = proj_groups(hp + 1, 1 - par)
                  if s == 0 and hp == 3:
                      late_consts()
                  attention_pair(hp, par, pending)
              if s == 0:
                  dbg_out("d_OG", OG[:])
              stop_at("B")

              if s > 0:
                  P.memset(POOL, Hin32[:], 0.0)
                  P.memset(POOL, Hin16[:], 0.0)
                  P.memset(POOL, carry[:], 0.0)
              ar = Arena()
              wpc = [ar.alloc((4, 8, 128), BF16) for _ in range(3)]
              wcnt = [0]
              YN = ar.alloc((16, 512), BF16)
              xs_tok0 = ar.alloc((4, 512), BF16)
              B_tok0 = ar.alloc((4, 128), BF16)
              szs0 = ar.alloc((4, 512), BF16)
              BT0 = ar.alloc((512,), BF16)
              CT0 = ar.alloc((512,), BF16)
              dts = []
              for _ in range(2):
                  dts.append(dict(dtv=ar.alloc((4, 32), F32), dA=ar.alloc((4, 32), F32), dA16=ar.alloc((4, 32), BF16),
                                  exps=ar.alloc((4, 96), F32), w1=ar.alloc((4, 32), F32), t32=ar.alloc((4, 32), F32)))
              mark = ar.off
              raw = [ar.alloc((516,), BF16) for _ in range(2)]
              dg = [ar.alloc((4, 128), BF16) for _ in range(2)]
              xsT = ar.alloc((4, 512), BF16)
              BT = [BT0, ar.alloc((512,), BF16)]
              CT = [CT0, ar.alloc((512,), BF16)]
              xs_tok = [xs_tok0, ar.alloc((4, 512), BF16)]
              B_tok = [B_tok0, ar.alloc((4, 128), BF16)]
              szs = [szs0, ar.alloc((4, 512), BF16)]
              rseg = ar.alloc((8, 128), BF16)
              E32 = ar.alloc((1024,), F32)
              Mb = ar.alloc((8, 128), BF16)
              CBm = ar.alloc((128,), F32)
              xdt = ar.alloc((512,), BF16)
              xw = ar.alloc((512,), BF16)
              xsD = ar.alloc((512,), BF16)
              y1 = ar.alloc((512,), F32)
              y2 = ar.alloc((512,), F32)
              y5 = ar.alloc((512,), BF16)
              jk = y5
              cmax = ar.off
              ar.off = mark
              MG = ar.alloc((8, 512), BF16)
              wd_ao = [ar.alloc((8, 128), BF16) for _ in range(2)]
              wd_so = [ar.alloc((16, 128), BF16) for _ in range(2)]
              wd_g = [ar.alloc((2, 8, 128), BF16) for _ in range(2)]
              wo_t = [ar.alloc((8, 512), BF16) for _ in range(2)]
              sg = [ar.alloc((512,), F32) for _ in range(2)]
              m1 = ar.alloc((512,), F32)
              m2 = ar.alloc((512,), F32)
              rr = ar.alloc((DM,), F32)
              fnw_bc = ar.alloc((DM,), F32)
              xt = [ar.alloc((DM,), F32) for _ in range(2)]
              ar.off = max(ar.off, cmax)
              pseg = bank(0, 2)
              pyd, pyo, pst, psm, ptr = bank(2), bank(0), bank(4), bank(5), bank(6)
              ppjs = [bank(7), bank(3)]
              ppjc = [0]
              ptr16 = ptr.bitcast(BF16)

              def load_piece(c0, n):
                  w = wpc[wcnt[0] % 3]
                  wcnt[0] += 1
                  P.dma(SP, w[:, 0:n], ws_in[c0:c0 + n].rearrange("c p k n -> p c k n"))
                  return w

              def c1_tasks(t0, g):
                  tasks = []
                  hold = {}

                  def gw(key, c0, n):
                      def f():
                          if key not in hold:
                              hold[key] = load_piece(c0, n)
                          return hold[key]
                      return f

                  def fm_conv(getw, wj, cc, dst, bi):
                      st_ = {}

                      def sa():
                          wt = getw()
                          st_["pb"] = ppjs[ppjc[0] % 2]
                          ppjc[0] += 1
                          for k in range(8):
                              P.mm(st_["pb"], wt[:, wj, k, :], hT[:, k, t0:t0 + 512], start=(k == 0), stop=(k == 7))

                      def sb_():
                          r_ = raw[bi]
                          P.copy(ACT, r_[:, 0:3], carry[:, cc, 0:3])
                          P.copy(DVE, r_[:, 3:515], st_["pb"])
                          P.copy(ACT, carry[:, cc, 0:3], r_[:, 512:515])
                          d_ = dg[bi]
                          for k in range(4):
                              P.act(d_[:, k, :], ident[:], AF.Identity, scale=convw[:, cc, k:k + 1])

                      def sc():
                          r_ = raw[bi]
                          d_ = dg[bi]
                          for k in range(4):
                              P.mm(st_["pb"], d_[:, k, :], r_[:, k:k + 512], start=(k == 0), stop=(k == 3))

                      def sd():
                          P.act(dst, st_["pb"], AF.Silu, bias=convb[:, cc:cc + 1])
                      return [sa, sb_, sc, sd]
                  lists = []
                  for j in range(4):
                      lists.append(fm_conv(gw("x", C_XS + g * 4, 4), j, g * 4 + j, xsT[:, j, :], j % 2))
                  lists.append(fm_conv(gw("b", C_B + g, 1), 0, 16 + g, BT[g % 2], 0))
                  lists.append(fm_conv(gw("c", C_C + g, 1), 0, 20 + g, CT[g % 2], 1))
                  for p0 in range(0, len(lists), 2):
                      for sidx in range(4):
                          for li in (p0, p0 + 1):
                              tasks.append(lists[li][sidx])
                  return tasks

              def dt_tasks(t0, D_):
                  tasks = []
                  hold = {}
                  lists = []

                  def chain(c):
                      st_ = {}
                      tk = slice(t0 + c * 128, t0 + (c + 1) * 128)
                      t32 = D_["t32"][:, c, :]

                      def s1():
                          if "w" not in hold:
                              hold["w"] = load_piece(C_DT, 1)
                          wdt = hold["w"]
                          st_["pb"] = ppjs[ppjc[0] % 2]
                          ppjc[0] += 1
                          for k in range(8):
                              P.mm(st_["pb"][:, 0:32], hT[:, k, tk], wdt[:, 0, k, 0:32], start=(k == 0), stop=(k == 7))

                      def s2():
                          P.tt(DVE, t32, st_["pb"][:, 0:32], dtb_bc[:], ALU.add)
                          P.act(t32, t32, AF.Exp)
                          P.act(D_["dtv"][:, c, :], t32, AF.Ln, bias=1.0)
                          P.tt(DVE, D_["dA"][:, c, :], D_["dtv"][:, c, :], a_bc[:], ALU.mult)
                          P.copy(DVE, D_["dA16"][:, c, :], D_["dA"][:, c, :])

                      def s3():
                          pb = st_["pb"]
                          P.mm(pb[:, 128:160], mLE[:], D_["dA16"][:, c, :])
                          P.mm(pb[:, 160:192], mTS[:], D_["dA16"][:, c, :])
                          P.mm(pb[:, 192:224], ones[:], D_["dA16"][:, c, :])

                      def s4():
                          P.act(D_["exps"][:, c, :], st_["pb"][:, 128:224], AF.Exp)
                          P.tt(DVE, D_["w1"][:, c, :], D_["exps"][:, c, 32:64], D_["dtv"][:, c, :], ALU.mult)
                      return [s1, s2, s3, s4]
                  for c in range(4):
                      lists.append(chain(c))
                  for p0 in range(0, 4, 2):
                      for sidx in range(4):
                          for li in (p0, p0 + 1):
                              tasks.append(lists[li][sidx])
                  return tasks

              for tt in range(4):
                  t0 = tt * 512
                  D_ = dts[tt % 2]
                  dtv, dA16, exps, w1 = D_["dtv"], D_["dA16"], D_["exps"], D_["w1"]

                  def c23_tasks(t0, g):
                      BTg = BT[g % 2]
                      xsk, Btk, szk = xs_tok[g % 2], B_tok[g % 2], szs[g % 2]
                      tasks = []
                      hold = {}

                      def c2a(c):
                          def t():
                              for j in range(4):
                                  P.tr(ptr16[:, j * 128:(j + 1) * 128], xsT[:, j, c * 128:(c + 1) * 128], ident[:])
                              P.tr(ptr16[:, 512:640], BTg[:, c * 128:(c + 1) * 128], ident[:])
                          return t

                      def c2b(c):
                          def t():
                              P.copy(DVE, xsk[:, c, :], ptr16[:, 0:512])
                              P.copy(DVE, Btk[:, c, :], ptr16[:, 512:640])
                          return t
                      st_ = {}

                      def c3a(c):
                          def t():
                              if "w" not in hold:
                                  hold["w"] = load_piece(C_ZS + g * 4, 4)
                              wz = hold["w"]
                              st_[c] = ppjs[ppjc[0] % 2]
                              ppjc[0] += 1
                              tk = slice(t0 + c * 128, t0 + (c + 1) * 128)
                              for k in range(8):
                                  P.mm(st_[c], hT[:, k, tk], wz[:, :, k, :], start=(k == 0), stop=(k == 7))
                          return t

                      def c3b(c):
                          def t():
                              P.act(szk[:, c, :], st_[c], AF.Silu)
                          return t
                      for c in range(4):
                          tasks.append(c3a(c))
                          if c > 0:
                              tasks.append(c3b(c - 1))
                      tasks.append(c3b(3))
                      for c in range(4):
                          tasks.append(c2a(c))
                          tasks.append(c2b(c))
                      return tasks

                  if tt == 0:
                      pend = dt_tasks(t0, D_) + c1_tasks(t0, 0) + c23_tasks(t0, 0)
                      while pend:
                          pend.pop(0)()
                      if s == 0:
                          dbg_out("d_dt", dtv[:, 0, :])
                          dbg_out("d_exps", exps[:, 0, :])
                  for g in range(4):
                      BTg, CTg = BT[g % 2], CT[g % 2]
                      xsk, Btk, szk = xs_tok[g % 2], B_tok[g % 2], szs[g % 2]
                      if s == 0 and tt == 0 and g == 0:
                          dbg_out("d_xsT", xsT)
                          dbg_out("d_BT", BTg)
                          dbg_out("d_xstok", xsk)
                          dbg_out("d_szs", szk)
                      if "noc23" in SKIP:
                          if g > 0:
                              for t_ in c23_tasks(t0, g):
                                  t_()
                          pend = c1_tasks(t0, g + 1) if g + 1 < 4 else []
                      elif g + 1 < 4:
                          pend = c1_tasks(t0, g + 1) + c23_tasks(t0, g + 1)
                      elif tt + 1 < 4:
                          pend = dt_tasks(t0 + 512, dts[(tt + 1) % 2]) + c1_tasks(t0 + 512, 0) + c23_tasks(t0 + 512, 0)
                      else:
                          pend = []
                      hs = slice(g * 8, (g + 1) * 8)
                      nslots = 22
                      per_slot = (len(pend) + nslots - 1) // nslots

                      def slot():
                          for _ in range(per_slot):
                              if pend:
                                  pend.pop(0)()
                      def cs_(c):
                          return slice(c * 128, (c + 1) * 128)

                      def xv_(c):
                          return xsk[:, c, :].rearrange("p (h d) -> p h d", h=8)

                      def pre_a(c):
                          P.mm(psm[:, 256:384], BTg[:, cs_(c)], CTg[:, cs_(c)])
                          P.tt(POOL, rseg, mLE[:].unsqueeze(1).broadcast_to([128, 8, 128]),
                               dA16[:, c, hs].unsqueeze(2).broadcast_to([128, 8, 128]), ALU.mult)
                          P.tt(DVE, CBm, psm[:, 256:384], mLE[:], ALU.mult)
                          for hf in range(2):
                              P.mm(pseg[:, hf * 512:(hf + 1) * 512], mTS[:], rseg[:, hf * 4:(hf + 1) * 4, :])
                          P.act(E32, pseg, AF.Exp)
                          P.tt(POOL, xdt.rearrange("p (h d) -> p h d", h=8), xv_(c),
                               dtv[:, c, hs].unsqueeze(2).broadcast_to([128, 8, 64]), ALU.mult)
                          P.tt(POOL, xsD.rearrange("p (h d) -> p h d", h=8), xv_(c),
                               dsk_bc[:, hs].unsqueeze(2).broadcast_to([128, 8, 64]), ALU.mult)
                          P.tt(POOL, xw.rearrange("p (h d) -> p h d", h=8), xv_(c),
                               w1[:, c, hs].unsqueeze(2).broadcast_to([128, 8, 64]), ALU.mult)

                      def pre_b(c):
                          P.tt(DVE, Mb, E32.rearrange("p (h l) -> p h l", h=8),
                               CBm.unsqueeze(1).broadcast_to([128, 8, 128]), ALU.mult)

                      def mid(c):
                          P.mm(pyd, ident[:], xsD, start=True, stop=True)
                          for h in range(8):
                              P.mm(pyd[:, h * 64:(h + 1) * 64], Mb[:, h, :], xdt[:, h * 64:(h + 1) * 64],
                                   start=False, stop=True, sgc=True)
                          P.mm(pyo, CTg[:, cs_(c)], Hin16[:, g, :])
                          P.mm(pst, Btk[:, c, :], xw)
                          P.tt(DVE, y1.rearrange("p (h d) -> p h d", h=8), pyo.rearrange("p (h d) -> p h d", h=8),
                               exps[:, c, g * 8:(g + 1) * 8].unsqueeze(2).broadcast_to([128, 8, 64]), ALU.mult)

                      def post_a(c):
                          P.tt(DVE, y2, y1, pyd, ALU.add)
                          P.tt(DVE, y1, y2, szk[:, c, :], ALU.mult)
                          P.act(jk, y1, AF.Square, accum_out=small[:, 4:5])
                          P.ts(DVE, small[:, 5:6], small[:, 4:5], 1.0 / 512, EPS, ALU.mult, ALU.add)
                          P.tt(POOL, small[:, 6:7], small[:, 5:6], nhalf[:], ALU.pow)

                      def post_b(c):
                          hv = Hin32[:, g, :].rearrange("p (h d) -> p h d", h=8)
                          P.tt(DVE, hv, hv, exps[:, c, 64 + g * 8:64 + (g + 1) * 8].unsqueeze(2).broadcast_to([128, 8, 64]),
                               ALU.mult)
                          P.tt(DVE, Hin32[:, g, :], Hin32[:, g, :], pst, ALU.add)
                          P.copy(ACT, Hin16[:, g, :], Hin32[:, g, :])
                          P.ts(DVE, y5, y1, small[:, 6:7], None, ALU.mult)
                          for j in range(4):
                              P.tr(ptr16[:, j * 128:(j + 1) * 128], y5[:, j * 128:(j + 1) * 128], ident[:])
                          for j in range(4):
                              P.act(YN[:, g * 4 + j, cs_(c)], ptr16[:, j * 128:(j + 1) * 128], AF.Identity,
                                    scale=snw_col[:, g * 4 + j:g * 4 + j + 1])

                      pre_a(0)
                      slot()
                      pre_b(0)
                      mid(0)
                      slot()
                      for c in range(4):
                          if c + 1 < 4:
                              pre_a(c + 1)
                          slot()
                          post_a(c)
                          slot()
                          if c + 1 < 4:
                              pre_b(c + 1)
                          slot()
                          post_b(c)
                          slot()
                          if c + 1 < 4:
                              mid(c + 1)
                          slot()
                      while pend:
                          pend.pop(0)()
                  if s == 0 and tt == 0:
                      dbg_out("d_YN", YN)
                  stop_at("C")

                  def load_d(dc):
                      i = dc % 2
                      P.dma(SP, wd_ao[i], ws_ao[dc])
                      P.dma(SP, wd_so[i], ws_so[dc])
                      P.dma(SP, wd_g[i][:, 0], ws_in[C_GA + dc])
                      P.dma(SP, wd_g[i][:, 1], ws_in[C_GS + dc])
                  P.dma(SP, fnw_bc, final_norm_w.broadcast_to([128, DM]))
                  load_d(0)
                  for dc in range(8):
                      i = dc % 2
                      if dc + 1 < 8:
                          load_d(dc + 1)
                      else:
                          for nh in range(2):
                              P.dma(SP, wo_t[nh], ws_o[nh])
                      pA, pB, pC, pD = [bank(4 * i + q) for q in range(4)]
                      for k in range(8):
                          P.mm(pC, wd_g[i][:, 0, k, :], hT[:, k, t0:t0 + 512], start=(k == 0), stop=(k == 7))
                      for k in range(8):
                          P.mm(pD, wd_g[i][:, 1, k, :], hT[:, k, t0:t0 + 512], start=(k == 0), stop=(k == 7))
                      for k in range(8):
                          P.mm(pA, wd_ao[i][:, k, :], OG[:, k, t0:t0 + 512], start=(k == 0), stop=(k == 7))
                      for k in range(16):
                          P.mm(pB, wd_so[i][:, k, :], YN[:, k, :], start=(k == 0), stop=(k == 15))
                      P.act(sg[0], pC, AF.Sigmoid)
                      P.act(sg[1], pD, AF.Sigmoid)
                      P.tt(DVE, m1, sg[0], pA, ALU.mult)
                      P.tt(DVE, m2, sg[1], pB, ALU.mult)
                      P.tt(DVE, MG[:, dc, :], m1, m2, ALU.add)
                  if s == 0 and tt == 0:
                      dbg_out("d_MG", MG)
                  for tb in range(4):
                      r0 = row0 + t0 + tb * 128
                      xi = xt[tb % 2]
                      P.dma(SP, xi, x[r0:r0 + 128, :])
                      for nh in range(2):
                          po = bank(2 * (tb % 2) + nh)
                          for k in range(8):
                              P.mm(po, MG[:, k, tb * 128:(tb + 1) * 128], wo_t[nh][:, k, :], start=(k == 0), stop=(k == 7))
                          P.tt(DVE, rr[:, nh * 512:(nh + 1) * 512], xi[:, nh * 512:(nh + 1) * 512], po, ALU.add)
                      P.act(xi, rr, AF.Square, accum_out=small[:, 8:9])
                      P.ts(DVE, small[:, 9:10], small[:, 8:9], 1.0 / DM, EPS, ALU.mult, ALU.add)
                      P.tt(POOL, small[:, 10:11], small[:, 9:10], nhalf[:], ALU.pow)
                      P.stt(xi, rr, small[:, 10:11], fnw_bc, ALU.mult, ALU.mult)
                      P.dma(SP, y[r0:r0 + 128, :], xi)
        try:
            main_body()
        except StopBuild:
            pass
        P.fence(SP, reads=[y])
        for nm in dbg:
            P.fence(SP, reads=[dbg[nm]])
        P.emit()
    return nc


_NC_CACHE = {}


def kernel(x, norm_w, w_in, conv_w, conv_b, dt_bias, a_log, d_skip, ssm_norm_w,
           w_attn_out, w_ssm_out, w_o, final_norm_w):
    ncores = 8
    if "nc" not in _NC_CACHE:
        _NC_CACHE["nc"] = build_program()
    nc = _NC_CACHE["nc"]
    f = lambda a: np.ascontiguousarray(np.asarray(a, dtype=np.float32))
    x = f(x)
    common = {
        "norm_w": f(norm_w).reshape(1, DM), "w_in": f(w_in).reshape(DM, 11296),
        "conv_w": f(conv_w).reshape(4, 3072), "conv_b": f(conv_b).reshape(1, 3072),
        "dt_bias": f(dt_bias).reshape(1, 32), "a_log": f(a_log).reshape(1, 32),
        "d_skip": f(d_skip).reshape(1, 32), "ssm_norm_w": f(ssm_norm_w).reshape(1, 2048),
        "w_attn_out": f(w_attn_out).reshape(1024, 1024), "w_ssm_out": f(w_ssm_out).reshape(2048, 1024),
        "w_o": f(w_o).reshape(1024, 1024), "final_norm_w": f(final_norm_w).reshape(1, DM),
    }
    in_maps = []
    for c in range(ncores):
        m = dict(common)
        m["x"] = x[c * NSEQ:(c + 1) * NSEQ].reshape(NSEQ * S, DM)
        in_maps.append(m)
    res = run_bass_kernel_spmd(nc, in_maps, core_ids=list(range(ncores)))
    out = np.concatenate([r["y"].reshape(NSEQ, S, DM) for r in res.results], axis=0)
    return out.astype(np.float32)
```

```python
import numpy as np
import concourse.bass as bass
import concourse.mybir as mybir

F32 = mybir.dt.float32
BF16 = mybir.dt.bfloat16
AF = mybir.ActivationFunctionType
ALU = mybir.AluOpType
AX = mybir.AxisListType

PE, ACT, DVE, POOL, SP = "pe", "act", "dve", "pool", "sp"
COMPUTE = (PE, ACT, DVE, POOL)
NDMASEM = 8
STRICT_SAME_ENGINE = [True]


def _region(ap):
    t = ap.tensor
    name = t.name
    pat = ap.ap
    off = int(ap.offset)
    es = mybir.dt.size(ap.dtype)
    space = str(ap.space)
    if "DRAM" in space.upper() or "HBM" in space.upper():
        ext = 0
        for (s, c) in pat:
            ext += (c - 1) * abs(s)
        return (name, 0, 1, off * es, (off + ext + 1) * es)
    prow = pat[0][0]
    pcnt = pat[0][1]
    if prow == 0:
        prow = 1 << 40
    p0 = off // prow
    f0 = off % prow
    ext = 0
    for (s, c) in pat[1:]:
        ext += (c - 1) * abs(s)
    if "PSUM" in space.upper():
        b0 = (f0 * es) // 2048 * 2048
        b1 = ((f0 + ext + 1) * es + 2047) // 2048 * 2048
        return (name, 0, 128, b0, b1)
    return (name, p0, p0 + pcnt, f0 * es, (f0 + ext + 1) * es)


class Op:
    __slots__ = ("eng", "fn", "idx", "deps", "sig", "signum", "is_dma", "dslot", "dval", "tag")

    def __init__(self, eng, fn, idx, is_dma=False, tag=""):
        self.eng = eng
        self.fn = fn
        self.idx = idx
        self.deps = {}
        self.sig = False
        self.signum = 0
        self.is_dma = is_dma
        self.dslot = None
        self.dval = 0
        self.tag = tag


class Prog:
    def __init__(self, nc):
        self.nc = nc
        self.ops = []
        self.track = {}
        self.untracked = set()
        self.dma_count = {}
        self.dma_rr = {SP: 0, POOL: 0, ACT: 0}

    def _key(self, op):
        if op.is_dma:
            return ("dma", op.eng, op.dslot)
        return op.eng

    def _add_dep(self, op, other_idx):
        o = self.ops[other_idx]
        k = self._key(o)
        if (not o.is_dma) and (not op.is_dma) and o.eng == op.eng and op.eng == PE:
            return
        cur = op.deps.get(k)
        if cur is None or cur < other_idx:
            op.deps[k] = other_idx

    def _access(self, op, ap, is_write, register=True):
        reg = _region(ap)
        name = reg[0]
        if name in self.untracked:
            return
        box = reg[1:]
        is_psum = (box[0] == 0 and box[1] == 128 and name.startswith("pbig"))
        ent = self.track.setdefault(name, {})
        p0, p1, f0, f1 = box
        dead = []
        for (b, k, w), idx in ent.items():
            if idx == op.idx:
                continue
            if b[0] < p1 and p0 < b[1] and b[2] < f1 and f0 < b[3]:
                if is_write or w or (is_psum and k != self._key(op)):
                    o = self.ops[idx]
                    same = (not o.is_dma) and (not op.is_dma) and o.eng == op.eng
                    if same and not STRICT_SAME_ENGINE[0] and not (w and not is_write):
                        pass
                    else:
                        self._add_dep(op, idx)
                if is_write and b[0] >= p0 and b[1] <= p1 and b[2] >= f0 and b[3] <= f1:
                    dead.append((b, k, w))
        if not register:
            return
        for d in dead:
            del ent[d]
        ent[(box, self._key(op), is_write)] = op.idx

    def add(self, eng, fn, reads=(), writes=(), is_dma=False, tag=""):
        op = Op(eng, fn, len(self.ops), is_dma=is_dma, tag=tag)
        if is_dma:
            slot = self.dma_rr[eng] % NDMASEM
            self.dma_rr[eng] += 1
            op.dslot = slot
            n = self.dma_count.get((eng, slot), 0) + 1
            self.dma_count[(eng, slot)] = n
            op.dval = 16 * n
        self.ops.append(op)
        reg = fn is not None
        for r in reads:
            if r is not None and not isinstance(r, (int, float)):
                self._access(op, r, False, reg)
        for w in writes:
            if w is not None:
                self._access(op, w, True, reg)
        for k, idx in op.deps.items():
            self.ops[idx].sig = True
        return op

    def mm(self, out, lhsT, rhs, start=True, stop=True, **kw):
        if kw.pop("sgc", False):
            kw["skip_group_check"] = True
        return self.add(PE, lambda e: e.matmul(out, lhsT, rhs, start=start, stop=stop, **kw),
                        reads=[lhsT, rhs], writes=[out], tag="mm")

    def tr(self, out, in_, ident):
        return self.add(PE, lambda e: e.transpose(out, in_, ident), reads=[in_, ident], writes=[out], tag="tr")

    def act(self, out, in_, func, bias=None, scale=None, accum_out=None, eng=ACT):
        kw = {}
        if bias is not None:
            kw["bias"] = bias
        if scale is not None:
            kw["scale"] = scale
        if accum_out is not None:
            kw["accum_out"] = accum_out
        return self.add(ACT, lambda e: e.activation(out, in_, func, **kw),
                        reads=[in_, bias, scale], writes=[out, accum_out], tag="act")

    def tt(self, eng, out, in0, in1, op):
        return self.add(eng, lambda e: e.tensor_tensor(out, in0, in1, op), reads=[in0, in1], writes=[out], tag="tt")

    def ts(self, eng, out, in0, s1, s2, op0, op1=None, accum_out=None):
        kw = {}
        if op1 is not None:
            kw["op1"] = op1
        if accum_out is not None:
            kw["accum_out"] = accum_out
        return self.add(eng, lambda e: e.tensor_scalar(out, in0, s1, s2, op0, **kw),
                        reads=[in0, s1, s2], writes=[out, accum_out], tag="ts")

    def stt(self, out, in0, scalar, in1, op0, op1, accum_out=None, eng=DVE):
        kw = {}
        if accum_out is not None:
            kw["accum_out"] = accum_out
        return self.add(eng, lambda e: e.scalar_tensor_tensor(out, in0, scalar, in1, op0, op1, **kw),
                        reads=[in0, scalar, in1], writes=[out, accum_out], tag="stt")

    def copy(self, eng, out, in_):
        if eng == ACT:
            return self.add(ACT, lambda e: e.copy(out, in_), reads=[in_], writes=[out], tag="copy")
        return self.add(eng, lambda e: e.tensor_copy(out, in_), reads=[in_], writes=[out], tag="copy")

    def memset(self, eng, ap, val):
        return self.add(eng, lambda e: e.memset(ap, val), writes=[ap], tag="memset")

    def dma(self, queue, out, in_, **kw):
        return self.add(queue, lambda e: e.dma_start(out, in_, **kw), reads=[in_], writes=[out], is_dma=True, tag="dma")

    def fence(self, eng, reads=(), writes=()):
        return self.add(eng, None, reads=reads, writes=writes, tag="fence")

    def emit(self):
        nc = self.nc
        cnt = {e: 0 for e in COMPUTE + (SP,)}
        for op in self.ops:
            if op.is_dma or op.fn is None:
                continue
            if op.sig:
                cnt[op.eng] += 1
                op.signum = cnt[op.eng]
        import contextlib
        with contextlib.ExitStack() as st:
            sems = {e: st.enter_context(nc.semaphore("s_" + e)) for e in COMPUTE}
            dsems = {}
            for q in (SP, POOL, ACT):
                for s in range(NDMASEM):
                    if (q, s) in self.dma_count:
                        dsems[(q, s)] = st.enter_context(nc.semaphore("d_%s%d" % (q, s)))
            block = st.enter_context(nc.Block())
            per_eng = {e: [] for e in COMPUTE + (SP,)}
            for op in self.ops:
                per_eng[op.eng].append(op)

            def run(engname, eng):
                waited = {}
                for op in per_eng[engname]:
                    for k, idx in op.deps.items():
                        o = self.ops[idx]
                        if o.is_dma:
                            sem = dsems[(o.eng, o.dslot)]
                            val = o.dval
                        else:
                            if o.fn is None:
                                continue
                            sem = sems[o.eng]
                            val = o.signum
                        if waited.get(k, 0) >= val:
                            continue
                        waited[k] = val
                        eng.wait_ge(sem, val)
                    if op.fn is None:
                        continue
                    if op.is_dma and op.dval > 16:
                        kk = ("dma", op.eng, op.dslot)
                        if waited.get(kk, 0) < op.dval - 16:
                            waited[kk] = op.dval - 16
                            eng.wait_ge(dsems[(op.eng, op.dslot)], op.dval - 16)
                    ins = op.fn(eng)
                    if op.is_dma:
                        ins.then_inc(dsems[(op.eng, op.dslot)], 16)
                    elif op.sig:
                        ins.then_inc(sems[op.eng], 1)

            @block.tensor
            def _(e):
                run(PE, e)

            @block.scalar
            def _(e):
                run(ACT, e)

            @block.vector
            def _(e):
                run(DVE, e)

            @block.gpsimd
            def _(e):
                run(POOL, e)

            @block.sync
            def _(e):
                run(SP, e)


import contextlib
from concourse.bass_utils import run_bass_kernel_spmd

S = 2048
DM = 1024
NSEQ = 2
EPS = 1e-6
CH = [(i * 128, 128) for i in range(72)] + [(9216, 32)] + [(9248 + i * 128, 128) for i in range(16)]
C_Q, C_K, C_V, C_ZA, C_ZS, C_XS, C_B, C_C, C_DT, C_GA, C_GS = 0, 8, 16, 24, 32, 48, 64, 68, 72, 73, 81
NCH = len(CH)

DEBUG = {}
STOP = [None]
SKIP = set()
FILL_N = [0]


class StopBuild(Exception):
    pass


def stop_at(tag):
    if STOP[0] == tag:
        raise StopBuild()


def build_program(nseq=NSEQ, debug=False):
    nc = bass.Bass("TRN2", target_bir_lowering=False)
    NT = nseq * S
    x = nc.dram_tensor("x", [NT, DM], F32, kind="ExternalInput").ap()
    norm_w = nc.dram_tensor("norm_w", [1, DM], F32, kind="ExternalInput").ap()
    w_in = nc.dram_tensor("w_in", [DM, 11296], F32, kind="ExternalInput").ap()
    conv_w = nc.dram_tensor("conv_w", [4, 3072], F32, kind="ExternalInput").ap()
    conv_b = nc.dram_tensor("conv_b", [1, 3072], F32, kind="ExternalInput").ap()
    dt_bias = nc.dram_tensor("dt_bias", [1, 32], F32, kind="ExternalInput").ap()
    a_log = nc.dram_tensor("a_log", [1, 32], F32, kind="ExternalInput").ap()
    d_skip = nc.dram_tensor("d_skip", [1, 32], F32, kind="ExternalInput").ap()
    ssm_norm_w = nc.dram_tensor("ssm_norm_w", [1, 2048], F32, kind="ExternalInput").ap()
    w_attn_out = nc.dram_tensor("w_attn_out", [1024, 1024], F32, kind="ExternalInput").ap()
    w_ssm_out = nc.dram_tensor("w_ssm_out", [2048, 1024], F32, kind="ExternalInput").ap()
    w_o = nc.dram_tensor("w_o", [1024, 1024], F32, kind="ExternalInput").ap()
    final_norm_w = nc.dram_tensor("final_norm_w", [1, DM], F32, kind="ExternalInput").ap()
    y = nc.dram_tensor("y", [NT, DM], F32, kind="ExternalOutput").ap()
    ws_in = nc.dram_tensor("ws_in", [NCH, 128, 8, 128], BF16, kind="Internal").ap()
    ws_ao = nc.dram_tensor("ws_ao", [8, 128, 8, 128], BF16, kind="Internal").ap()
    ws_so = nc.dram_tensor("ws_so", [8, 128, 16, 128], BF16, kind="Internal").ap()
    ws_o = nc.dram_tensor("ws_o", [2, 128, 8, 512], BF16, kind="Internal").ap()
    dbg = {}
    if debug:
        for nm, shp, dt_ in debug:
            dbg[nm] = nc.dram_tensor(nm, shp, dt_, kind="ExternalOutput").ap()

    with contextlib.ExitStack() as st:
        def sb(n, s, d):
            return st.enter_context(nc.sbuf_tensor(n, s, d))
        P = Prog(nc)
        for nm in ("x", "norm_w", "w_in", "conv_w", "conv_b", "dt_bias", "a_log", "d_skip", "ssm_norm_w",
                   "w_attn_out", "w_ssm_out", "w_o", "final_norm_w"):
            P.untracked.add(nm)

        hT = sb("hT", [128, 8, S], BF16)
        OG = sb("OG", [128, 8, S], BF16)
        ident = sb("ident", [128, 128], BF16)
        mT = sb("mT", [128, 128], BF16)
        mLo = sb("mLo", [128, 128], BF16)
        mLE = sb("mLE", [128, 128], BF16)
        mTS = sb("mTS", [128, 128], BF16)
        ones = sb("ones", [128, 128], BF16)
        mNeg = sb("mNeg", [128, 128], BF16)
        zeros = sb("zeros", [128, 128], BF16)
        dtb_bc = sb("dtb_bc", [128, 32], F32)
        a_bc = sb("a_bc", [128, 32], F32)
        dsk_bc = sb("dsk_bc", [128, 32], F32)
        convw = sb("convw", [128, 24, 4], F32)
        convb = sb("convb", [128, 24], F32)
        nhalf = sb("nhalf", [128, 1], F32)
        snw_col = sb("snw_col", [128, 16], F32)
        Hin32 = sb("Hin32", [128, 4, 512], F32)
        Hin16 = sb("Hin16", [128, 4, 512], BF16)
        carry = sb("carry", [128, 24, 4], BF16)
        small = sb("small", [128, 16], F32)
        arena_t = sb("arena", [128, 64 * 1024], BF16)
        pbig = st.enter_context(nc.psum_tensor("pbig", [128, 4096], F32))

        def bank(i, n=1):
            return pbig[:, i * 512:(i + n) * 512]

        class Arena:
            def __init__(self):
                self.off = 0

            def alloc(self, shape, dtype):
                n = 1
                for v in shape:
                    n *= v
                nb = n * (4 if dtype == F32 else 2)
                ne = (nb // 2 + 15) // 16 * 16
                v = arena_t[:, self.off:self.off + nb // 2]
                self.off += ne
                assert self.off <= 64 * 1024, self.off
                if dtype == F32:
                    v = v.bitcast(F32)
                if len(shape) == 2:
                    v = v.rearrange("p (a b) -> p a b", a=shape[0])
                elif len(shape) == 3:
                    v = v.rearrange("p (a b c) -> p a b c", a=shape[0], b=shape[1])
                return v

        def mask(t, op, sgn=1, base=0):
            P.memset(POOL, t[:], 1.0)
            P.add(POOL, lambda e: e.affine_select(t[:], t[:], [[-sgn, 128]], op, 0.0, base=base, channel_multiplier=sgn),
                  reads=[t[:]], writes=[t[:]])
        mask(ident, ALU.is_equal)
        mask(mT, ALU.is_ge, 1, 0)
        mask(mLo, ALU.is_ge, -1, -1)
        mask(mLE, ALU.is_ge, -1, 0)
        mask(mTS, ALU.is_ge, 1, -1)
        P.memset(POOL, ones[:], 1.0)
        P.memset(POOL, mNeg[:], -30000.0)
        P.add(POOL, lambda e: e.affine_select(mNeg[:], mNeg[:], [[-1, 128]], ALU.is_ge, 0.0, base=0, channel_multiplier=1),
              reads=[mNeg[:]], writes=[mNeg[:]])
        P.memset(POOL, zeros[:], 0.0)
        P.memset(POOL, nhalf[:], -0.5)
        P.memset(POOL, Hin32[:], 0.0)
        P.memset(POOL, Hin16[:], 0.0)
        P.memset(POOL, carry[:], 0.0)
        def late_consts():
            P.dma(SP, dtb_bc[:], dt_bias.broadcast_to([128, 32]))
            P.dma(SP, a_bc[:], a_log.broadcast_to([128, 32]))
            P.dma(SP, dsk_bc[:], d_skip.broadcast_to([128, 32]))
            cwv = conv_w.rearrange("k (c p) -> k c p", p=128)
            cbv = conv_b.rearrange("o (c p) -> o c p", p=128)
            for c in range(24):
                for k in range(4):
                    P.dma(SP, convw[:, c, k:k + 1], cwv[k, c, :].unsqueeze(1))
                P.dma(SP, convb[:, c:c + 1], cbv[0, c, :].unsqueeze(1))
            snv = ssm_norm_w.rearrange("o (c p) -> o c p", p=128)
            for c in range(16):
                P.dma(SP, snw_col[:, c:c + 1], snv[0, c, :].unsqueeze(1))
            P.act(a_bc[:], a_bc[:], AF.Exp)
            P.ts(DVE, a_bc[:], a_bc[:], -1.0, None, ALU.mult)

        w_in_v = w_in.rearrange("(k p) n -> p k n", p=128)

        def conv_chunks(c0, n):
            col0 = CH[c0][0]
            wd = CH[c0][1]
            if wd == 128:
                for i in range(n):
                    P.dma(POOL, ws_in[c0 + i], w_in_v[:, :, col0 + i * 128:col0 + (i + 1) * 128])
            else:
                P.dma(POOL, ws_in[c0], w_in_v[:, :, col0:col0 + 128])
        for hp in range(8):
            for base in (C_Q, C_K, C_V, C_ZA):
                conv_chunks(base + hp, 1)
        for c0 in range(C_ZS, C_DT, 2):
            conv_chunks(c0, 2)
        conv_chunks(C_DT, 1)
        for c0 in range(C_GA, NCH, 2):
            conv_chunks(c0, 2)
        wao_v = w_attn_out.rearrange("(k p) n -> p k n", p=128)
        wso_v = w_ssm_out.rearrange("(k p) n -> p k n", p=128)
        wo_v = w_o.rearrange("(k p) n -> p k n", p=128)
        for dc in range(8):
            P.dma(POOL, ws_ao[dc], wao_v[:, :, dc * 128:(dc + 1) * 128])
        for dc in range(8):
            P.dma(POOL, ws_so[dc], wso_v[:, :, dc * 128:(dc + 1) * 128])
        for nh in range(2):
            P.dma(POOL, ws_o[nh], wo_v[:, :, nh * 512:(nh + 1) * 512])

        def dbg_out(name, src_ap, dst_slice=None):
            if name in dbg:
                d = dbg[name] if dst_slice is None else dst_slice(dbg[name])
                P.dma(SP, d, src_ap)

        trb = bank(7).bitcast(BF16)

        def main_body():
          stop_at("init")
          for s in range(nseq):
              row0 = s * S
              ar = Arena()
              junk = ar.alloc((DM,), BF16)
              hb = [ar.alloc((DM,), BF16) for _ in range(2)]
              normw_bc = ar.alloc((DM,), F32)
              xt = [ar.alloc((DM,), F32) for _ in range(2)]
              P.dma(SP, normw_bc, norm_w.broadcast_to([128, DM]))
              for i in range(16):
                  xi = xt[i % 2]
                  so = (i % 2) * 12
                  trb_ = bank(7 - (i % 2)).bitcast(BF16)
                  P.dma(SP, xi, x[row0 + i * 128: row0 + (i + 1) * 128, :])
                  ss = small[:, so:so + 1]
                  P.act(junk, xi, AF.Square, accum_out=ss)
                  P.ts(DVE, small[:, so + 1:so + 2], ss, 1.0 / DM, EPS, ALU.mult, ALU.add)
                  P.act(small[:, so + 3:so + 4], small[:, so + 1:so + 2], AF.Ln)
                  P.act(small[:, so + 2:so + 3], small[:, so + 3:so + 4], AF.Exp, scale=-0.5)
                  h_ = hb[i % 2]
                  P.stt(h_, xi, small[:, so + 2:so + 3], normw_bc, ALU.mult, ALU.mult)
                  for k in range(8):
                      P.tr(trb_[:, k * 128:(k + 1) * 128], h_[:, k * 128:(k + 1) * 128], ident[:])
                  P.copy(ACT, hT[:, :, i * 128:(i + 1) * 128], trb_.rearrange("p (k t) -> p k t", k=8))
              if s == 0:
                  dbg_out("d_hT", hT[:])
              stop_at("A")

              ar = Arena()
              qT = [ar.alloc((S,), BF16) for _ in range(2)]
              ksp = [[ar.alloc((S,), BF16) for _ in range(2)] for _ in range(2)]
              sz = [ar.alloc((S,), BF16) for _ in range(2)]
              Vp = [[ar.alloc((16, 128), BF16) for _ in range(2)] for _ in range(2)]
              Wb = [ar.alloc((4, 8, 128), BF16) for _ in range(2)]
              e32 = [[ar.alloc((512,), F32) for _ in range(2)] for _ in range(2)]
              spb = [[ar.alloc((512,), BF16) for _ in range(2)] for _ in range(2)]
              Pm = [[ar.alloc((512,), F32) for _ in range(2)] for _ in range(2)]
              ab = [[ar.alloc((512,), BF16) for _ in range(2)] for _ in range(2)]
              for par_ in range(2):
                  P.memset(DVE, ksp[par_][0][64:128, :], 0.0)
                  P.memset(DVE, ksp[par_][1][0:64, :], 0.0)
                  P.memset(DVE, Vp[par_][0][:, :, 64:128], 0.0)
                  P.memset(DVE, Vp[par_][1][:, :, 0:64], 0.0)
              zb = [bank(0), bank(1)]
              acc = [bank(2), bank(3)]
              ob = [bank(4), bank(5)]
              pj = [bank(6), bank(7)]
              pjc = [0]

              def load_w(hp, par):
                  for gi, base in enumerate((C_Q, C_K, C_V, C_ZA)):
                      P.dma(SP, Wb[par][:, gi], ws_in[base + hp])

              def proj_groups(hp, par):
                  micro = []
                  W = Wb[par]

                  def fm(gi, tt, kind):
                      pbi = pjc[0] % 2
                      pjc[0] += 1
                      pb = pj[pbi]
                      for k in range(8):
                          micro.append(lambda k=k: P.mm(pb, W[:, gi, k, :], hT[:, k, tt * 512:(tt + 1) * 512],
                                                        start=(k == 0), stop=(k == 7)))
                      sl = slice(tt * 512, (tt + 1) * 512)

                      def ev():
                          if kind == "q":
                              P.copy(DVE, qT[par][:, sl], pb)
                          elif kind == "k":
                              P.ts(DVE, ksp[par][0][0:64, sl], pb[0:64, :], 0.125, None, ALU.mult)
                              P.ts(DVE, ksp[par][1][64:128, sl], pb[64:128, :], 0.125, None, ALU.mult)
                          else:
                              P.act(sz[par][:, sl], pb, AF.Silu)
                      micro.append(ev)

                  def vg(vb):
                      pbi = pjc[0] % 2
                      pjc[0] += 1
                      pb = pj[pbi]
                      for j in range(4):
                          tb = vb * 4 + j
                          for k in range(0, 8, 2):
                              def mm2(j=j, tb=tb, k=k):
                                  for kk in (k, k + 1):
                                      P.mm(pb[:, j * 128:(j + 1) * 128], hT[:, kk, tb * 128:(tb + 1) * 128], W[:, 2, kk, :],
                                           start=(kk == 0), stop=(kk == 7))
                              micro.append(mm2)

                      def ev():
                          pv = pb.rearrange("p (j n) -> p j n", j=4)
                          P.copy(DVE, Vp[par][0][:, vb * 4:(vb + 1) * 4, 0:64], pv[:, :, 0:64])
                          P.copy(DVE, Vp[par][1][:, vb * 4:(vb + 1) * 4, 64:128], pv[:, :, 64:128])
                      micro.append(ev)
                  for tt in range(4):
                      fm(0, tt, "q")
                      fm(1, tt, "k")
                      vg(tt)
                  for tt in range(4):
                      fm(3, tt, "za")
                  return micro

              def attention_pair(hp, par, pending):
                  steps = [(qc, sbk) for qc in range(4) for sbk in range(4 * qc + 3, -1, -1)]

                  def geo(qc, sbk):
                      c0 = (sbk - 4 * qc) * 128 if sbk >= 4 * qc else 0
                      return c0, (sbk >= 4 * qc)

                  def emit_z(th, i):
                      qc, sbk = steps[i]
                      c0, dg_ = geo(qc, sbk)
                      P.mm(zb[th][:, c0:512], ksp[par][th][:, sbk * 128:(sbk + 1) * 128],
                           qT[par][:, qc * 512 + c0:(qc + 1) * 512], start=True, stop=True)
                      if dg_:
                          P.mm(zb[th][:, c0:c0 + 128], ident[:], mNeg[:], start=False, stop=True, sgc=True)
                  n = len(steps)

                  def fill():
                      for _ in range(FILL_N[0]):
                          P.mm(ob[1], ident[:], qT[par][:, 0:512], start=True, stop=True)

                  def S1(th, i):
                      qc, sbk = steps[i]
                      c0, diag = geo(qc, sbk)
                      bi = i % 2
                      if sbk == 4 * qc + 3:
                          P.mm(acc[th], zeros[:], qT[par][:, 0:512], start=True, stop=True)
                      e_, s_ = e32[th][bi], spb[th][bi]
                      P.act(e_[:, c0:512], zb[th][:, c0:512], AF.Exp)
                      P.act(s_[:, c0:512], e_[:, c0:512], AF.Ln, bias=1.0)
                      fill()
                      P.mm(acc[th][:, c0:512], mT[:], s_[:, c0:512], start=False, stop=True, sgc=True)
                      if i + 1 < n:
                          emit_z(th, i + 1)

                  def S2(th, i):
                      qc, sbk = steps[i]
                      c0, diag = geo(qc, sbk)
                      bi = i % 2
                      obk = ob[qc % 2] if FILL_N[0] == 0 else ob[0]
                      e_, s_, p_, a_ = e32[th][bi], spb[th][bi], Pm[th][bi], ab[th][bi]
                      if th == 0 and sbk == 4 * qc + 3:
                          P.mm(obk, zeros[:], qT[par][:, 0:512], start=True, stop=True)
                      P.act(p_[:, c0:512], acc[th][:, c0:512], AF.Exp, scale=-1.0)
                      P.tt(DVE, a_[:, c0:512], p_[:, c0:512], e_[:, c0:512], ALU.mult)
                      fill()
                      if sbk != 0:
                          P.mm(acc[th][:, c0:512], mLo[:], s_[:, c0:512], start=False, stop=True, sgc=True)
                      P.mm(obk[:, c0:512], Vp[par][th][:, sbk, :], a_[:, c0:512], start=False, stop=True, sgc=True)
                      if th == 1 and sbk == 0:
                          P.tt(DVE, OG[:, hp, qc * 512:(qc + 1) * 512], obk, sz[par][:, qc * 512:(qc + 1) * 512], ALU.mult)

                  nmicro = len(pending)
                  per = (nmicro + 4 * n - 1) // (4 * n) if nmicro else 0

                  def pop():
                      for _ in range(per):
                          if pending:
                              pending.pop(0)()
                  for th in range(2):
                      emit_z(th, 0)
                  S1(0, 0)
                  for i in range(n):
                      S1(1, i)
                      pop()
                      S2(0, i)
                      pop()
                      if i + 1 < n:
                          S1(0, i + 1)
                      pop()
                      S2(1, i)
                      pop()
                  while pending:
                      pending.pop(0)()

              if "B" not in SKIP:
                  load_w(0, 0)
                  for g in proj_groups(0, 0):
                      g()
              for hp in range(8 if "B" not in SKIP else 0):
                  par = hp % 2
                  pending = []
                  if hp + 1 < 8:
                      load_w(hp + 1, 1 - par)
                      pending = proj_groups(hp + 1, 1 - par)
                  if s == 0 and hp == 3:
                      late_consts()
                  attention_pair(hp, par, pending)
              if s == 0:
                  dbg_out("d_OG", OG[:])
              stop_at("B")

              if s > 0:
                  P.memset(POOL, Hin32[:], 0.0)
                  P.memset(POOL, Hin16[:], 0.0)
                  P.memset(POOL, carry[:], 0.0)
              ar = Arena()
              wpc = [ar.alloc((4, 8, 128), BF16) for _ in range(3)]
              wcnt = [0]
              YN = ar.alloc((16, 512), BF16)
              xs_tok0 = ar.alloc((4, 512), BF16)
              B_tok0 = ar.alloc((4, 128), BF16)
              szs0 = ar.alloc((4, 512), BF16)
              BT0 = ar.alloc((512,), BF16)
              CT0 = ar.alloc((512,), BF16)
              dts = []
              for _ in range(2):
                  dts.append(dict(dtv=ar.alloc((4, 32), F32), dA=ar.alloc((4, 32), F32), dA16=ar.alloc((4, 32), BF16),
                                  exps=ar.alloc((4, 96), F32), w1=ar.alloc((4, 32), F32), t32=ar.alloc((4, 32), F32)))
              mark = ar.off
              raw = [ar.alloc((516,), BF16) for _ in range(2)]
              dg = [ar.alloc((4, 128), BF16) for _ in range(2)]
              xsT = ar.alloc((4, 512), BF16)
              BT = [BT0, ar.alloc((512,), BF16)]
              CT = [CT0, ar.alloc((512,), BF16)]
              xs_tok = [xs_tok0, ar.alloc((4, 512), BF16)]
              B_tok = [B_tok0, ar.alloc((4, 128), BF16)]
              szs = [szs0, ar.alloc((4, 512), BF16)]
              rseg = ar.alloc((8, 128), BF16)
              E32 = ar.alloc((1024,), F32)
              Mb = ar.alloc((8, 128), BF16)
              CBm = ar.alloc((128,), F32)
              xdt = ar.alloc((512,), BF16)
              xw = ar.alloc((512,), BF16)
              xsD = ar.alloc((512,), BF16)
              y1 = ar.alloc((512,), F32)
              y2 = ar.alloc((512,), F32)
              y5 = ar.alloc((512,), BF16)
              jk = y5
              cmax = ar.off
              ar.off = mark
              MG = ar.alloc((8, 512), BF16)
              wd_ao = [ar.alloc((8, 128), BF16) for _ in range(2)]
              wd_so = [ar.alloc((16, 128), BF16) for _ in range(2)]
              wd_g = [ar.alloc((2, 8, 128), BF16) for _ in range(2)]
              wo_t = [ar.alloc((8, 512), BF16) for _ in range(2)]
              sg = [ar.alloc((512,), F32) for _ in range(2)]
              m1 = ar.alloc((512,), F32)
              m2 = ar.alloc((512,), F32)
              rr = ar.alloc((DM,), F32)
              fnw_bc = ar.alloc((DM,), F32)
              xt = [ar.alloc((DM,), F32) for _ in range(2)]
              ar.off = max(ar.off, cmax)
              pseg = bank(0, 2)
              pyd, pyo, pst, psm, ptr = bank(2), bank(0), bank(4), bank(5), bank(6)
              ppjs = [bank(7), bank(3)]
              ppjc = [0]
              ptr16 = ptr.bitcast(BF16)

              def load_piece(c0, n):
                  w = wpc[wcnt[0] % 3]
                  wcnt[0] += 1
                  P.dma(SP, w[:, 0:n], ws_in[c0:c0 + n].rearrange("c p k n -> p c k n"))
                  return w

              def c1_tasks(t0, g):
                  tasks = []
                  hold = {}

                  def gw(key, c0, n):
                      def f():
                          if key not in hold:
                              hold[key] = load_piece(c0, n)
                          return hold[key]
                      return f

                  def fm_conv(getw, wj, cc, dst, bi):
                      st_ = {}

                      def sa():
                          wt = getw()
                          st_["pb"] = ppjs[ppjc[0] % 2]
                          ppjc[0] += 1
                          for k in range(8):
                              P.mm(st_["pb"], wt[:, wj, k, :], hT[:, k, t0:t0 + 512], start=(k == 0), stop=(k == 7))

                      def sb_():
                          r_ = raw[bi]
                          P.copy(ACT, r_[:, 0:3], carry[:, cc, 0:3])
                          P.copy(DVE, r_[:, 3:515], st_["pb"])
                          P.copy(ACT, carry[:, cc, 0:3], r_[:, 512:515])
                          d_ = dg[bi]
                          for k in range(4):
                              P.act(d_[:, k, :], ident[:], AF.Identity, scale=convw[:, cc, k:k + 1])

                      def sc():
                          r_ = raw[bi]
                          d_ = dg[bi]
                          for k in range(4):
                              P.mm(st_["pb"], d_[:, k, :], r_[:, k:k + 512], start=(k == 0), stop=(k == 3))

                      def sd():
                          P.act(dst, st_["pb"], AF.Silu, bias=convb[:, cc:cc + 1])
                      return [sa, sb_, sc, sd]
                  lists = []
                  for j in range(4):
                      lists.append(fm_conv(gw("x", C_XS + g * 4, 4), j, g * 4 + j, xsT[:, j, :], j % 2))
                  lists.append(fm_conv(gw("b", C_B + g, 1), 0, 16 + g, BT[g % 2], 0))
                  lists.append(fm_conv(gw("c", C_C + g, 1), 0, 20 + g, CT[g % 2], 1))
                  for p0 in range(0, len(lists), 2):
                      for sidx in range(4):
                          for li in (p0, p0 + 1):
                              tasks.append(lists[li][sidx])
                  return tasks

              def dt_tasks(t0, D_):
                  tasks = []
                  hold = {}
                  lists = []

                  def chain(c):
                      st_ = {}
                      tk = slice(t0 + c * 128, t0 + (c + 1) * 128)
                      t32 = D_["t32"][:, c, :]

                      def s1():
                          if "w" not in hold:
                              hold["w"] = load_piece(C_DT, 1)
                          wdt = hold["w"]
                          st_["pb"] = ppjs[ppjc[0] % 2]
                          ppjc[0] += 1
                          for k in range(8):
                              P.mm(st_["pb"][:, 0:32], hT[:, k, tk], wdt[:, 0, k, 0:32], start=(k == 0), stop=(k == 7))

                      def s2():
                          P.tt(DVE, t32, st_["pb"][:, 0:32], dtb_bc[:], ALU.add)
                          P.act(t32, t32, AF.Exp)
                          P.act(D_["dtv"][:, c, :], t32, AF.Ln, bias=1.0)
                          P.tt(DVE, D_["dA"][:, c, :], D_["dtv"][:, c, :], a_bc[:], ALU.mult)
                          P.copy(DVE, D_["dA16"][:, c, :], D_["dA"][:, c, :])

                      def s3():
                          pb = st_["pb"]
                          P.mm(pb[:, 128:160], mLE[:], D_["dA16"][:, c, :])
                          P.mm(pb[:, 160:192], mTS[:], D_["dA16"][:, c, :])
                          P.mm(pb[:, 192:224], ones[:], D_["dA16"][:, c, :])

                      def s4():
                          P.act(D_["exps"][:, c, :], st_["pb"][:, 128:224], AF.Exp)
                          P.tt(DVE, D_["w1"][:, c, :], D_["exps"][:, c, 32:64], D_["dtv"][:, c, :], ALU.mult)
                      return [s1, s2, s3, s4]
                  for c in range(4):
                      lists.append(chain(c))
                  for p0 in range(0, 4, 2):
                      for sidx in range(4):
                          for li in (p0, p0 + 1):
                              tasks.append(lists[li][sidx])
                  return tasks

              for tt in range(4):
                  t0 = tt * 512
                  D_ = dts[tt % 2]
                  dtv, dA16, exps, w1 = D_["dtv"], D_["dA16"], D_["exps"], D_["w1"]

                  def c23_tasks(t0, g):
                      BTg = BT[g % 2]
                      xsk, Btk, szk = xs_tok[g % 2], B_tok[g % 2], szs[g % 2]
                      tasks = []
                      hold = {}

                      def c2a(c):
                          def t():
                              for j in range(4):
                                  P.tr(ptr16[:, j * 128:(j + 1) * 128], xsT[:, j, c * 128:(c + 1) * 128], ident[:])
                              P.tr(ptr16[:, 512:640], BTg[:, c * 128:(c + 1) * 128], ident[:])
                          return t

                      def c2b(c):
                          def t():
                              P.copy(DVE, xsk[:, c, :], ptr16[:, 0:512])
                              P.copy(DVE, Btk[:, c, :], ptr16[:, 512:640])
                          return t
                      st_ = {}

                      def c3a(c):
                          def t():
                              if "w" not in hold:
                                  hold["w"] = load_piece(C_ZS + g * 4, 4)
                              wz = hold["w"]
                              st_[c] = ppjs[ppjc[0] % 2]
                              ppjc[0] += 1
                              tk = slice(t0 + c * 128, t0 + (c + 1) * 128)
                              for k in range(8):
                                  P.mm(st_[c], hT[:, k, tk], wz[:, :, k, :], start=(k == 0), stop=(k == 7))
                          return t

                      def c3b(c):
                          def t():
                              P.act(szk[:, c, :], st_[c], AF.Silu)
                          return t
                      for c in range(4):
                          tasks.append(c3a(c))
                          if c > 0:
                              tasks.append(c3b(c - 1))
                      tasks.append(c3b(3))
                      for c in range(4):
                          tasks.append(c2a(c))
                          tasks.append(c2b(c))
                      return tasks

                  if tt == 0:
                      pend = dt_tasks(t0, D_) + c1_tasks(t0, 0) + c23_tasks(t0, 0)
                      while pend:
                          pend.pop(0)()
                      if s == 0:
                          dbg_out("d_dt", dtv[:, 0, :])
                          dbg_out("d_exps", exps[:, 0, :])
                  for g in range(4):
                      BTg, CTg = BT[g % 2], CT[g % 2]
                      xsk, Btk, szk = xs_tok[g % 2], B_tok[g % 2], szs[g % 2]
                      if s == 0 and tt == 0 and g == 0:
                          dbg_out("d_xsT", xsT)
                          dbg_out("d_BT", BTg)
                          dbg_out("d_xstok", xsk)
                          dbg_out("d_szs", szk)
                      if "noc23" in SKIP:
                          if g > 0:
                              for t_ in c23_tasks(t0, g):
                                  t_()
                          pend = c1_tasks(t0, g + 1) if g + 1 < 4 else []
                      elif g + 1 < 4:
                          pend = c1_tasks(t0, g + 1) + c23_tasks(t0, g + 1)
                      elif tt + 1 < 4:
                          pend = dt_tasks(t0 + 512, dts[(tt + 1) % 2]) + c1_tasks(t0 + 512, 0) + c23_tasks(t0 + 512, 0)
                      else:
                          pend = []
                      hs = slice(g * 8, (g + 1) * 8)
                      nslots = 22
                      per_slot = (len(pend) + nslots - 1) // nslots

                      def slot():
                          for _ in range(per_slot):
                              if pend:
                                  pend.pop(0)()
                      def cs_(c):
                          return slice(c * 128, (c + 1) * 128)

                      def xv_(c):
                          return xsk[:, c, :].rearrange("p (h d) -> p h d", h=8)

                      def pre_a(c):
                          P.mm(psm[:, 256:384], BTg[:, cs_(c)], CTg[:, cs_(c)])
                          P.tt(POOL, rseg, mLE[:].unsqueeze(1).broadcast_to([128, 8, 128]),
                               dA16[:, c, hs].unsqueeze(2).broadcast_to([128, 8, 128]), ALU.mult)
                          P.tt(DVE, CBm, psm[:, 256:384], mLE[:], ALU.mult)
                          for hf in range(2):
                              P.mm(pseg[:, hf * 512:(hf + 1) * 512], mTS[:], rseg[:, hf * 4:(hf + 1) * 4, :])
                          P.act(E32, pseg, AF.Exp)
                          P.tt(POOL, xdt.rearrange("p (h d) -> p h d", h=8), xv_(c),
                               dtv[:, c, hs].unsqueeze(2).broadcast_to([128, 8, 64]), ALU.mult)
                          P.tt(POOL, xsD.rearrange("p (h d) -> p h d", h=8), xv_(c),
                               dsk_bc[:, hs].unsqueeze(2).broadcast_to([128, 8, 64]), ALU.mult)
                          P.tt(POOL, xw.rearrange("p (h d) -> p h d", h=8), xv_(c),
                               w1[:, c, hs].unsqueeze(2).broadcast_to([128, 8, 64]), ALU.mult)

                      def pre_b(c):
                          P.tt(DVE, Mb, E32.rearrange("p (h l) -> p h l", h=8),
                               CBm.unsqueeze(1).broadcast_to([128, 8, 128]), ALU.mult)

                      def mid(c):
                          P.mm(pyd, ident[:], xsD, start=True, stop=True)
                          for h in range(8):
                              P.mm(pyd[:, h * 64:(h + 1) * 64], Mb[:, h, :], xdt[:, h * 64:(h + 1) * 64],
                                   start=False, stop=True, sgc=True)
                          P.mm(pyo, CTg[:, cs_(c)], Hin16[:, g, :])
                          P.mm(pst, Btk[:, c, :], xw)
                          P.tt(DVE, y1.rearrange("p (h d) -> p h d", h=8), pyo.rearrange("p (h d) -> p h d", h=8),
                               exps[:, c, g * 8:(g + 1) * 8].unsqueeze(2).broadcast_to([128, 8, 64]), ALU.mult)

                      def post_a(c):
                          P.tt(DVE, y2, y1, pyd, ALU.add)
                          P.tt(DVE, y1, y2, szk[:, c, :], ALU.mult)
                          P.act(jk, y1, AF.Square, accum_out=small[:, 4:5])
                          P.ts(DVE, small[:, 5:6], small[:, 4:5], 1.0 / 512, EPS, ALU.mult, ALU.add)
                          P.tt(POOL, small[:, 6:7], small[:, 5:6], nhalf[:], ALU.pow)

                      def post_b(c):
                          hv = Hin32[:, g, :].rearrange("p (h d) -> p h d", h=8)
                          P.tt(DVE, hv, hv, exps[:, c, 64 + g * 8:64 + (g + 1) * 8].unsqueeze(2).broadcast_to([128, 8, 64]),
                               ALU.mult)
                          P.tt(DVE, Hin32[:, g, :], Hin32[:, g, :], pst, ALU.add)
                          P.copy(ACT, Hin16[:, g, :], Hin32[:, g, :])
                          P.ts(DVE, y5, y1, small[:, 6:7], None, ALU.mult)
                          for j in range(4):
                              P.tr(ptr16[:, j * 128:(j + 1) * 128], y5[:, j * 128:(j + 1) * 128], ident[:])
                          for j in range(4):
                              P.act(YN[:, g * 4 + j, cs_(c)], ptr16[:, j * 128:(j + 1) * 128], AF.Identity,
                                    scale=snw_col[:, g * 4 + j:g * 4 + j + 1])

                      pre_a(0)
                      slot()
                      pre_b(0)
                      mid(0)
                      slot()
                      for c in range(4):
                          if c + 1 < 4:
                              pre_a(c + 1)
                          slot()
                          post_a(c)
                          slot()
                          if c + 1 < 4:
                              pre_b(c + 1)
                          slot()
                          post_b(c)
                          slot()
                          if c + 1 < 4:
                              mid(c + 1)
                          slot()
                      while pend:
                          pend.pop(0)()
                  if s == 0 and tt == 0:
                      dbg_out("d_YN", YN)
                  stop_at("C")

                  def load_d(dc):
                      i = dc % 2
                      P.dma(SP, wd_ao[i], ws_ao[dc])
                      P.dma(SP, wd_so[i], ws_so[dc])
                      P.dma(SP, wd_g[i][:, 0], ws_in[C_GA + dc])
                      P.dma(SP, wd_g[i][:, 1], ws_in[C_GS + dc])
                  P.dma(SP, fnw_bc, final_norm_w.broadcast_to([128, DM]))
                  load_d(0)
                  for dc in range(8):
                      i = dc % 2
                      if dc + 1 < 8:
                          load_d(dc + 1)
                      else:
                          for nh in range(2):
                              P.dma(SP, wo_t[nh], ws_o[nh])
                      pA, pB, pC, pD = [bank(4 * i + q) for q in range(4)]
                      for k in range(8):
                          P.mm(pC, wd_g[i][:, 0, k, :], hT[:, k, t0:t0 + 512], start=(k == 0), stop=(k == 7))
                      for k in range(8):
                          P.mm(pD, wd_g[i][:, 1, k, :], hT[:, k, t0:t0 + 512], start=(k == 0), stop=(k == 7))
                      for k in range(8):
                          P.mm(pA, wd_ao[i][:, k, :], OG[:, k, t0:t0 + 512], start=(k == 0), stop=(k == 7))
                      for k in range(16):
                          P.mm(pB, wd_so[i][:, k, :], YN[:, k, :], start=(k == 0), stop=(k == 15))
                      P.act(sg[0], pC, AF.Sigmoid)
                      P.act(sg[1], pD, AF.Sigmoid)
                      P.tt(DVE, m1, sg[0], pA, ALU.mult)
                      P.tt(DVE, m2, sg[1], pB, ALU.mult)
                      P.tt(DVE, MG[:, dc, :], m1, m2, ALU.add)
                  if s == 0 and tt == 0:
                      dbg_out("d_MG", MG)
                  for tb in range(4):
                      r0 = row0 + t0 + tb * 128
                      xi = xt[tb % 2]
                      P.dma(SP, xi, x[r0:r0 + 128, :])
                      for nh in range(2):
                          po = bank(2 * (tb % 2) + nh)
                          for k in range(8):
                              P.mm(po, MG[:, k, tb * 128:(tb + 1) * 128], wo_t[nh][:, k, :], start=(k == 0), stop=(k == 7))
                          P.tt(DVE, rr[:, nh * 512:(nh + 1) * 512], xi[:, nh * 512:(nh + 1) * 512], po, ALU.add)
                      P.act(xi, rr, AF.Square, accum_out=small[:, 8:9])
                      P.ts(DVE, small[:, 9:10], small[:, 8:9], 1.0 / DM, EPS, ALU.mult, ALU.add)
                      P.tt(POOL, small[:, 10:11], small[:, 9:10], nhalf[:], ALU.pow)
                      P.stt(xi, rr, small[:, 10:11], fnw_bc, ALU.mult, ALU.mult)
                      P.dma(SP, y[r0:r0 + 128, :], xi)
        try:
            main_body()
        except StopBuild:
            pass
        P.fence(SP, reads=[y])
        for nm in dbg:
            P.fence(SP, reads=[dbg[nm]])
        P.emit()
    return nc


_NC_CACHE = {}


def kernel(x, norm_w, w_in, conv_w, conv_b, dt_bias, a_log, d_skip, ssm_norm_w,
           w_attn_out, w_ssm_out, w_o, final_norm_w):
    ncores = 8
    if "nc" not in _NC_CACHE:
        _NC_CACHE["nc"] = build_program()
    nc = _NC_CACHE["nc"]
    f = lambda a: np.ascontiguousarray(np.asarray(a, dtype=np.float32))
    x = f(x)
    common = {
        "norm_w": f(norm_w).reshape(1, DM), "w_in": f(w_in).reshape(DM, 11296),
        "conv_w": f(conv_w).reshape(4, 3072), "conv_b": f(conv_b).reshape(1, 3072),
        "dt_bias": f(dt_bias).reshape(1, 32), "a_log": f(a_log).reshape(1, 32),
        "d_skip": f(d_skip).reshape(1, 32), "ssm_norm_w": f(ssm_norm_w).reshape(1, 2048),
        "w_attn_out": f(w_attn_out).reshape(1024, 1024), "w_ssm_out": f(w_ssm_out).reshape(2048, 1024),
        "w_o": f(w_o).reshape(1024, 1024), "final_norm_w": f(final_norm_w).reshape(1, DM),
    }
    in_maps = []
    for c in range(ncores):
        m = dict(common)
        m["x"] = x[c * NSEQ:(c + 1) * NSEQ].reshape(NSEQ * S, DM)
        in_maps.append(m)
    res = run_bass_kernel_spmd(nc, in_maps, core_ids=list(range(ncores)))
    out = np.concatenate([r["y"].reshape(NSEQ, S, DM) for r in res.results], axis=0)
    return out.astype(np.float32)
```

```python
import numpy as np
import concourse.bass as bass
import concourse.mybir as mybir

F32 = mybir.dt.float32
BF16 = mybir.dt.bfloat16
AF = mybir.ActivationFunctionType
ALU = mybir.AluOpType
AX = mybir.AxisListType

PE, ACT, DVE, POOL, SP = "pe", "act", "dve", "pool", "sp"
COMPUTE = (PE, ACT, DVE, POOL)
NDMASEM = 8
STRICT_SAME_ENGINE = [True]


def _region(ap):
    t = ap.tensor
    name = t.name
    pat = ap.ap
    off = int(ap.offset)
    es = mybir.dt.size(ap.dtype)
    space = str(ap.space)
    if "DRAM" in space.upper() or "HBM" in space.upper():
        ext = 0
        for (s, c) in pat:
            ext += (c - 1) * abs(s)
        return (name, 0, 1, off * es, (off + ext + 1) * es)
    prow = pat[0][0]
    pcnt = pat[0][1]
    if prow == 0:
        prow = 1 << 40
    p0 = off // prow
    f0 = off % prow
    ext = 0
    for (s, c) in pat[1:]:
        ext += (c - 1) * abs(s)
    if "PSUM" in space.upper():
        b0 = (f0 * es) // 2048 * 2048
        b1 = ((f0 + ext + 1) * es + 2047) // 2048 * 2048
        return (name, 0, 128, b0, b1)
    return (name, p0, p0 + pcnt, f0 * es, (f0 + ext + 1) * es)


class Op:
    __slots__ = ("eng", "fn", "idx", "deps", "sig", "signum", "is_dma", "dslot", "dval", "tag")

    def __init__(self, eng, fn, idx, is_dma=False, tag=""):
        self.eng = eng
        self.fn = fn
        self.idx = idx
        self.deps = {}
        self.sig = False
        self.signum = 0
        self.is_dma = is_dma
        self.dslot = None
        self.dval = 0
        self.tag = tag


class Prog:
    def __init__(self, nc):
        self.nc = nc
        self.ops = []
        self.track = {}
        self.untracked = set()
        self.dma_count = {}
        self.dma_rr = {SP: 0, POOL: 0, ACT: 0}

    def _key(self, op):
        if op.is_dma:
            return ("dma", op.eng, op.dslot)
        return op.eng

    def _add_dep(self, op, other_idx):
        o = self.ops[other_idx]
        k = self._key(o)
        if (not o.is_dma) and (not op.is_dma) and o.eng == op.eng and op.eng == PE:
            return
        cur = op.deps.get(k)
        if cur is None or cur < other_idx:
            op.deps[k] = other_idx

    def _access(self, op, ap, is_write, register=True):
        reg = _region(ap)
        name = reg[0]
        if name in self.untracked:
            return
        box = reg[1:]
        is_psum = (box[0] == 0 and box[1] == 128 and name.startswith("pbig"))
        ent = self.track.setdefault(name, {})
        p0, p1, f0, f1 = box
        dead = []
        for (b, k, w), idx in ent.items():
            if idx == op.idx:
                continue
            if b[0] < p1 and p0 < b[1] and b[2] < f1 and f0 < b[3]:
                if is_write or w or (is_psum and k != self._key(op)):
                    o = self.ops[idx]
                    same = (not o.is_dma) and (not op.is_dma) and o.eng == op.eng
                    if same and not STRICT_SAME_ENGINE[0] and not (w and not is_write):
                        pass
                    else:
                        self._add_dep(op, idx)
                if is_write and b[0] >= p0 and b[1] <= p1 and b[2] >= f0 and b[3] <= f1:
                    dead.append((b, k, w))
        if not register:
            return
        for d in dead:
            del ent[d]
        ent[(box, self._key(op), is_write)] = op.idx

    def add(self, eng, fn, reads=(), writes=(), is_dma=False, tag=""):
        op = Op(eng, fn, len(self.ops), is_dma=is_dma, tag=tag)
        if is_dma:
            slot = self.dma_rr[eng] % NDMASEM
            self.dma_rr[eng] += 1
            op.dslot = slot
            n = self.dma_count.get((eng, slot), 0) + 1
            self.dma_count[(eng, slot)] = n
            op.dval = 16 * n
        self.ops.append(op)
        reg = fn is not None
        for r in reads:
            if r is not None and not isinstance(r, (int, float)):
                self._access(op, r, False, reg)
        for w in writes:
            if w is not None:
                self._access(op, w, True, reg)
        for k, idx in op.deps.items():
            self.ops[idx].sig = True
        return op

    def mm(self, out, lhsT, rhs, start=True, stop=True, **kw):
        if kw.pop("sgc", False):
            kw["skip_group_check"] = True
        return self.add(PE, lambda e: e.matmul(out, lhsT, rhs, start=start, stop=stop, **kw),
                        reads=[lhsT, rhs], writes=[out], tag="mm")

    def tr(self, out, in_, ident):
        return self.add(PE, lambda e: e.transpose(out, in_, ident), reads=[in_, ident], writes=[out], tag="tr")

    def act(self, out, in_, func, bias=None, scale=None, accum_out=None, eng=ACT):
        kw = {}
        if bias is not None:
            kw["bias"] = bias
        if scale is not None:
            kw["scale"] = scale
        if accum_out is not None:
            kw["accum_out"] = accum_out
        return self.add(ACT, lambda e: e.activation(out, in_, func, **kw),
                        reads=[in_, bias, scale], writes=[out, accum_out], tag="act")

    def tt(self, eng, out, in0, in1, op):
        return self.add(eng, lambda e: e.tensor_tensor(out, in0, in1, op), reads=[in0, in1], writes=[out], tag="tt")

    def ts(self, eng, out, in0, s1, s2, op0, op1=None, accum_out=None):
        kw = {}
        if op1 is not None:
            kw["op1"] = op1
        if accum_out is not None:
            kw["accum_out"] = accum_out
        return self.add(eng, lambda e: e.tensor_scalar(out, in0, s1, s2, op0, **kw),
                        reads=[in0, s1, s2], writes=[out, accum_out], tag="ts")

    def stt(self, out, in0, scalar, in1, op0, op1, accum_out=None, eng=DVE):
        kw = {}
        if accum_out is not None:
            kw["accum_out"] = accum_out
        return self.add(eng, lambda e: e.scalar_tensor_tensor(out, in0, scalar, in1, op0, op1, **kw),
                        reads=[in0, scalar, in1], writes=[out, accum_out], tag="stt")

    def copy(self, eng, out, in_):
        if eng == ACT:
            return self.add(ACT, lambda e: e.copy(out, in_), reads=[in_], writes=[out], tag="copy")
        return self.add(eng, lambda e: e.tensor_copy(out, in_), reads=[in_], writes=[out], tag="copy")

    def memset(self, eng, ap, val):
        return self.add(eng, lambda e: e.memset(ap, val), writes=[ap], tag="memset")

    def dma(self, queue, out, in_, **kw):
        return self.add(queue, lambda e: e.dma_start(out, in_, **kw), reads=[in_], writes=[out], is_dma=True, tag="dma")

    def fence(self, eng, reads=(), writes=()):
        return self.add(eng, None, reads=reads, writes=writes, tag="fence")

    def emit(self):
        nc = self.nc
        cnt = {e: 0 for e in COMPUTE + (SP,)}
        for op in self.ops:
            if op.is_dma or op.fn is None:
                continue
            if op.sig:
                cnt[op.eng] += 1
                op.signum = cnt[op.eng]
        import contextlib
        with contextlib.ExitStack() as st:
            sems = {e: st.enter_context(nc.semaphore("s_" + e)) for e in COMPUTE}
            dsems = {}
            for q in (SP, POOL, ACT):
                for s in range(NDMASEM):
                    if (q, s) in self.dma_count:
                        dsems[(q, s)] = st.enter_context(nc.semaphore("d_%s%d" % (q, s)))
            block = st.enter_context(nc.Block())
            per_eng = {e: [] for e in COMPUTE + (SP,)}
            for op in self.ops:
                per_eng[op.eng].append(op)

            def run(engname, eng):
                waited = {}
                for op in per_eng[engname]:
                    for k, idx in op.deps.items():
                        o = self.ops[idx]
                        if o.is_dma:
                            sem = dsems[(o.eng, o.dslot)]
                            val = o.dval
                        else:
                            if o.fn is None:
                                continue
                            sem = sems[o.eng]
                            val = o.signum
                        if waited.get(k, 0) >= val:
                            continue
                        waited[k] = val
                        eng.wait_ge(sem, val)
                    if op.fn is None:
                        continue
                    if op.is_dma and op.dval > 16:
                        kk = ("dma", op.eng, op.dslot)
                        if waited.get(kk, 0) < op.dval - 16:
                            waited[kk] = op.dval - 16
                            eng.wait_ge(dsems[(op.eng, op.dslot)], op.dval - 16)
                    ins = op.fn(eng)
                    if op.is_dma:
                        ins.then_inc(dsems[(op.eng, op.dslot)], 16)
                    elif op.sig:
                        ins.then_inc(sems[op.eng], 1)

            @block.tensor
            def _(e):
                run(PE, e)

            @block.scalar
            def _(e):
                run(ACT, e)

            @block.vector
            def _(e):
                run(DVE, e)

            @block.gpsimd
            def _(e):
                run(POOL, e)

            @block.sync
            def _(e):
                run(SP, e)


import contextlib
from concourse.bass_utils import run_bass_kernel_spmd

S = 2048
DM = 1024
NSEQ = 2
EPS = 1e-6
CH = [(i * 128, 128) for i in range(72)] + [(9216, 32)] + [(9248 + i * 128, 128) for i in range(16)]
C_Q, C_K, C_V, C_ZA, C_ZS, C_XS, C_B, C_C, C_DT, C_GA, C_GS = 0, 8, 16, 24, 32, 48, 64, 68, 72, 73, 81
NCH = len(CH)

DEBUG = {}
STOP = [None]
SKIP = set()
FILL_N = [0]


class StopBuild(Exception):
    pass


def stop_at(tag):
    if STOP[0] == tag:
        raise StopBuild()


def build_program(nseq=NSEQ, debug=False):
    nc = bass.Bass("TRN2", target_bir_lowering=False)
    NT = nseq * S
    x = nc.dram_tensor("x", [NT, DM], F32, kind="ExternalInput").ap()
    norm_w = nc.dram_tensor("norm_w", [1, DM], F32, kind="ExternalInput").ap()
    w_in = nc.dram_tensor("w_in", [DM, 11296], F32, kind="ExternalInput").ap()
    conv_w = nc.dram_tensor("conv_w", [4, 3072], F32, kind="ExternalInput").ap()
    conv_b = nc.dram_tensor("conv_b", [1, 3072], F32, kind="ExternalInput").ap()
    dt_bias = nc.dram_tensor("dt_bias", [1, 32], F32, kind="ExternalInput").ap()
    a_log = nc.dram_tensor("a_log", [1, 32], F32, kind="ExternalInput").ap()
    d_skip = nc.dram_tensor("d_skip", [1, 32], F32, kind="ExternalInput").ap()
    ssm_norm_w = nc.dram_tensor("ssm_norm_w", [1, 2048], F32, kind="ExternalInput").ap()
    w_attn_out = nc.dram_tensor("w_attn_out", [1024, 1024], F32, kind="ExternalInput").ap()
    w_ssm_out = nc.dram_tensor("w_ssm_out", [2048, 1024], F32, kind="ExternalInput").ap()
    w_o = nc.dram_tensor("w_o", [1024, 1024], F32, kind="ExternalInput").ap()
    final_norm_w = nc.dram_tensor("final_norm_w", [1, DM], F32, kind="ExternalInput").ap()
    y = nc.dram_tensor("y", [NT, DM], F32, kind="ExternalOutput").ap()
    ws_in = nc.dram_tensor("ws_in", [NCH, 128, 8, 128], BF16, kind="Internal").ap()
    ws_ao = nc.dram_tensor("ws_ao", [8, 128, 8, 128], BF16, kind="Internal").ap()
    ws_so = nc.dram_tensor("ws_so", [8, 128, 16, 128], BF16, kind="Internal").ap()
    ws_o = nc.dram_tensor("ws_o", [2, 128, 8, 512], BF16, kind="Internal").ap()
    dbg = {}
    if debug:
        for nm, shp, dt_ in debug:
            dbg[nm] = nc.dram_tensor(nm, shp, dt_, kind="ExternalOutput").ap()

    with contextlib.ExitStack() as st:
        def sb(n, s, d):
            return st.enter_context(nc.sbuf_tensor(n, s, d))
        P = Prog(nc)
        for nm in ("x", "norm_w", "w_in", "conv_w", "conv_b", "dt_bias", "a_log", "d_skip", "ssm_norm_w",
                   "w_attn_out", "w_ssm_out", "w_o", "final_norm_w"):
            P.untracked.add(nm)

        hT = sb("hT", [128, 8, S], BF16)
        OG = sb("OG", [128, 8, S], BF16)
        ident = sb("ident", [128, 128], BF16)
        mT = sb("mT", [128, 128], BF16)
        mLo = sb("mLo", [128, 128], BF16)
        mLE = sb("mLE", [128, 128], BF16)
        mTS = sb("mTS", [128, 128], BF16)
        ones = sb("ones", [128, 128], BF16)
        mNeg = sb("mNeg", [128, 128], BF16)
        zeros = sb("zeros", [128, 128], BF16)
        dtb_bc = sb("dtb_bc", [128, 32], F32)
        a_bc = sb("a_bc", [128, 32], F32)
        dsk_bc = sb("dsk_bc", [128, 32], F32)
        convw = sb("convw", [128, 24, 4], F32)
        convb = sb("convb", [128, 24], F32)
        nhalf = sb("nhalf", [128, 1], F32)
        snw_col = sb("snw_col", [128, 16], F32)
        Hin32 = sb("Hin32", [128, 4, 512], F32)
        Hin16 = sb("Hin16", [128, 4, 512], BF16)
        carry = sb("carry", [128, 24, 4], BF16)
        small = sb("small", [128, 16], F32)
        arena_t = sb("arena", [128, 64 * 1024], BF16)
        pbig = st.enter_context(nc.psum_tensor("pbig", [128, 4096], F32))

        def bank(i, n=1):
            return pbig[:, i * 512:(i + n) * 512]

        class Arena:
            def __init__(self):
                self.off = 0

            def alloc(self, shape, dtype):
                n = 1
                for v in shape:
                    n *= v
                nb = n * (4 if dtype == F32 else 2)
                ne = (nb // 2 + 15) // 16 * 16
                v = arena_t[:, self.off:self.off + nb // 2]
                self.off += ne
                assert self.off <= 64 * 1024, self.off
                if dtype == F32:
                    v = v.bitcast(F32)
                if len(shape) == 2:
                    v = v.rearrange("p (a b) -> p a b", a=shape[0])
                elif len(shape) == 3:
                    v = v.rearrange("p (a b c) -> p a b c", a=shape[0], b=shape[1])
                return v

        def mask(t, op, sgn=1, base=0):
            P.memset(POOL, t[:], 1.0)
            P.add(POOL, lambda e: e.affine_select(t[:], t[:], [[-sgn, 128]], op, 0.0, base=base, channel_multiplier=sgn),
                  reads=[t[:]], writes=[t[:]])
        mask(ident, ALU.is_equal)
        mask(mT, ALU.is_ge, 1, 0)
        mask(mLo, ALU.is_ge, -1, -1)
        mask(mLE, ALU.is_ge, -1, 0)
        mask(mTS, ALU.is_ge, 1, -1)
        P.memset(POOL, ones[:], 1.0)
        P.memset(POOL, mNeg[:], -30000.0)
        P.add(POOL, lambda e: e.affine_select(mNeg[:], mNeg[:], [[-1, 128]], ALU.is_ge, 0.0, base=0, channel_multiplier=1),
              reads=[mNeg[:]], writes=[mNeg[:]])
        P.memset(POOL, zeros[:], 0.0)
        P.memset(POOL, nhalf[:], -0.5)
        P.memset(POOL, Hin32[:], 0.0)
        P.memset(POOL, Hin16[:], 0.0)
        P.memset(POOL, carry[:], 0.0)
        def late_consts():
            P.dma(SP, dtb_bc[:], dt_bias.broadcast_to([128, 32]))
            P.dma(SP, a_bc[:], a_log.broadcast_to([128, 32]))
            P.dma(SP, dsk_bc[:], d_skip.broadcast_to([128, 32]))
            cwv = conv_w.rearrange("k (c p) -> k c p", p=128)
            cbv = conv_b.rearrange("o (c p) -> o c p", p=128)
            for c in range(24):
                for k in range(4):
                    P.dma(SP, convw[:, c, k:k + 1], cwv[k, c, :].unsqueeze(1))
                P.dma(SP, convb[:, c:c + 1], cbv[0, c, :].unsqueeze(1))
            snv = ssm_norm_w.rearrange("o (c p) -> o c p", p=128)
            for c in range(16):
                P.dma(SP, snw_col[:, c:c + 1], snv[0, c, :].unsqueeze(1))
            P.act(a_bc[:], a_bc[:], AF.Exp)
            P.ts(DVE, a_bc[:], a_bc[:], -1.0, None, ALU.mult)

        w_in_v = w_in.rearrange("(k p) n -> p k n", p=128)

        def conv_chunks(c0, n):
            col0 = CH[c0][0]
            wd = CH[c0][1]
            if wd == 128:
                for i in range(n):
                    P.dma(POOL, ws_in[c0 + i], w_in_v[:, :, col0 + i * 128:col0 + (i + 1) * 128])
            else:
                P.dma(POOL, ws_in[c0], w_in_v[:, :, col0:col0 + 128])
        for hp in range(8):
            for base in (C_Q, C_K, C_V, C_ZA):
                conv_chunks(base + hp, 1)
        for c0 in range(C_ZS, C_DT, 2):
            conv_chunks(c0, 2)
        conv_chunks(C_DT, 1)
        for c0 in range(C_GA, NCH, 2):
            conv_chunks(c0, 2)
        wao_v = w_attn_out.rearrange("(k p) n -> p k n", p=128)
        wso_v = w_ssm_out.rearrange("(k p) n -> p k n", p=128)
        wo_v = w_o.rearrange("(k p) n -> p k n", p=128)
        for dc in range(8):
            P.dma(POOL, ws_ao[dc], wao_v[:, :, dc * 128:(dc + 1) * 128])
        for dc in range(8):
            P.dma(POOL, ws_so[dc], wso_v[:, :, dc * 128:(dc + 1) * 128])
        for nh in range(2):
            P.dma(POOL, ws_o[nh], wo_v[:, :, nh * 512:(nh + 1) * 512])

        def dbg_out(name, src_ap, dst_slice=None):
            if name in dbg:
                d = dbg[name] if dst_slice is None else dst_slice(dbg[name])
                P.dma(SP, d, src_ap)

        trb = bank(7).bitcast(BF16)

        def main_body():
          stop_at("init")
          for s in range(nseq):
              row0 = s * S
              ar = Arena()
              junk = ar.alloc((DM,), BF16)
              hb = [ar.alloc((DM,), BF16) for _ in range(2)]
              normw_bc = ar.alloc((DM,), F32)
              xt = [ar.alloc((DM,), F32) for _ in range(2)]
              P.dma(SP, normw_bc, norm_w.broadcast_to([128, DM]))
              for i in range(16):
                  xi = xt[i % 2]
                  so = (i % 2) * 12
                  trb_ = bank(7 - (i % 2)).bitcast(BF16)
                  P.dma(SP, xi, x[row0 + i * 128: row0 + (i + 1) * 128, :])
                  ss = small[:, so:so + 1]
                  P.act(junk, xi, AF.Square, accum_out=ss)
                  P.ts(DVE, small[:, so + 1:so + 2], ss, 1.0 / DM, EPS, ALU.mult, ALU.add)
                  P.act(small[:, so + 3:so + 4], small[:, so + 1:so + 2], AF.Ln)
                  P.act(small[:, so + 2:so + 3], small[:, so + 3:so + 4], AF.Exp, scale=-0.5)
                  h_ = hb[i % 2]
                  P.stt(h_, xi, small[:, so + 2:so + 3], normw_bc, ALU.mult, ALU.mult)
                  for k in range(8):
                      P.tr(trb_[:, k * 128:(k + 1) * 128], h_[:, k * 128:(k + 1) * 128], ident[:])
                  P.copy(ACT, hT[:, :, i * 128:(i + 1) * 128], trb_.rearrange("p (k t) -> p k t", k=8))
              if s == 0:
                  dbg_out("d_hT", hT[:])
              stop_at("A")

              ar = Arena()
              qT = [ar.alloc((S,), BF16) for _ in range(2)]
              ksp = [[ar.alloc((S,), BF16) for _ in range(2)] for _ in range(2)]
              sz = [ar.alloc((S,), BF16) for _ in range(2)]
              Vp = [[ar.alloc((16, 128), BF16) for _ in range(2)] for _ in range(2)]
              Wb = [ar.alloc((4, 8, 128), BF16) for _ in range(2)]
              e32 = [[ar.alloc((512,), F32) for _ in range(2)] for _ in range(2)]
              spb = [[ar.alloc((512,), BF16) for _ in range(2)] for _ in range(2)]
              Pm = [[ar.alloc((512,), F32) for _ in range(2)] for _ in range(2)]
              ab = [[ar.alloc((512,), BF16) for _ in range(2)] for _ in range(2)]
              for par_ in range(2):
                  P.memset(DVE, ksp[par_][0][64:128, :], 0.0)
                  P.memset(DVE, ksp[par_][1][0:64, :], 0.0)
                  P.memset(DVE, Vp[par_][0][:, :, 64:128], 0.0)
                  P.memset(DVE, Vp[par_][1][:, :, 0:64], 0.0)
              zb = [bank(0), bank(1)]
              acc = [bank(2), bank(3)]
              ob = [bank(4), bank(5)]
              pj = [bank(6), bank(7)]
              pjc = [0]

              def load_w(hp, par):
                  for gi, base in enumerate((C_Q, C_K, C_V, C_ZA)):
                      P.dma(SP, Wb[par][:, gi], ws_in[base + hp])

              def proj_groups(hp, par):
                  micro = []
                  W = Wb[par]

                  def fm(gi, tt, kind):
                      pbi = pjc[0] % 2
                      pjc[0] += 1
                      pb = pj[pbi]
                      for k in range(8):
                          micro.append(lambda k=k: P.mm(pb, W[:, gi, k, :], hT[:, k, tt * 512:(tt + 1) * 512],
                                                        start=(k == 0), stop=(k == 7)))
                      sl = slice(tt * 512, (tt + 1) * 512)

                      def ev():
                          if kind == "q":
                              P.copy(DVE, qT[par][:, sl], pb)
                          elif kind == "k":
                              P.ts(DVE, ksp[par][0][0:64, sl], pb[0:64, :], 0.125, None, ALU.mult)
                              P.ts(DVE, ksp[par][1][64:128, sl], pb[64:128, :], 0.125, None, ALU.mult)
                          else:
                              P.act(sz[par][:, sl], pb, AF.Silu)
                      micro.append(ev)

                  def vg(vb):
                      pbi = pjc[0] % 2
                      pjc[0] += 1
                      pb = pj[pbi]
                      for j in range(4):
                          tb = vb * 4 + j
                          for k in range(0, 8, 2):
                              def mm2(j=j, tb=tb, k=k):
                                  for kk in (k, k + 1):
                                      P.mm(pb[:, j * 128:(j + 1) * 128], hT[:, kk, tb * 128:(tb + 1) * 128], W[:, 2, kk, :],
                                           start=(kk == 0), stop=(kk == 7))
                              micro.append(mm2)

                      def ev():
                          pv = pb.rearrange("p (j n) -> p j n", j=4)
                          P.copy(DVE, Vp[par][0][:, vb * 4:(vb + 1) * 4, 0:64], pv[:, :, 0:64])
                          P.copy(DVE, Vp[par][1][:, vb * 4:(vb + 1) * 4, 64:128], pv[:, :, 64:128])
                      micro.append(ev)
                  for tt in range(4):
                      fm(0, tt, "q")
                      fm(1, tt, "k")
                      vg(tt)
                  for tt in range(4):
                      fm(3, tt, "za")
                  return micro

              def attention_pair(hp, par, pending):
                  steps = [(qc, sbk) for qc in range(4) for sbk in range(4 * qc + 3, -1, -1)]

                  def geo(qc, sbk):
                      c0 = (sbk - 4 * qc) * 128 if sbk >= 4 * qc else 0
                      return c0, (sbk >= 4 * qc)

                  def emit_z(th, i):
                      qc, sbk = steps[i]
                      c0, dg_ = geo(qc, sbk)
                      P.mm(zb[th][:, c0:512], ksp[par][th][:, sbk * 128:(sbk + 1) * 128],
                           qT[par][:, qc * 512 + c0:(qc + 1) * 512], start=True, stop=True)
                      if dg_:
                          P.mm(zb[th][:, c0:c0 + 128], ident[:], mNeg[:], start=False, stop=True, sgc=True)
                  n = len(steps)

                  def fill():
                      for _ in range(FILL_N[0]):
                          P.mm(ob[1], ident[:], qT[par][:, 0:512], start=True, stop=True)

                  def S1(th, i):
                      qc, sbk = steps[i]
                      c0, diag = geo(qc, sbk)
                      bi = i % 2
                      if sbk == 4 * qc + 3:
                          P.mm(acc[th], zeros[:], qT[par][:, 0:512], start=True, stop=True)
                      e_, s_ = e32[th][bi], spb[th][bi]
                      P.act(e_[:, c0:512], zb[th][:, c0:512], AF.Exp)
                      P.act(s_[:, c0:512], e_[:, c0:512], AF.Ln, bias=1.0)
                      fill()
                      P.mm(acc[th][:, c0:512], mT[:], s_[:, c0:512], start=False, stop=True, sgc=True)
                      if i + 1 < n:
                          emit_z(th, i + 1)

                  def S2(th, i):
                      qc, sbk = steps[i]
                      c0, diag = geo(qc, sbk)
                      bi = i % 2
                      obk = ob[qc % 2] if FILL_N[0] == 0 else ob[0]
                      e_, s_, p_, a_ = e32[th][bi], spb[th][bi], Pm[th][bi], ab[th][bi]
                      if th == 0 and sbk == 4 * qc + 3:
                          P.mm(obk, zeros[:], qT[par][:, 0:512], start=True, stop=True)
                      P.act(p_[:, c0:512], acc[th][:, c0:512], AF.Exp, scale=-1.0)
                      P.tt(DVE, a_[:, c0:512], p_[:, c0:512], e_[:, c0:512], ALU.mult)
                      fill()
                      if sbk != 0:
                          P.mm(acc[th][:, c0:512], mLo[:], s_[:, c0:512], start=False, stop=True, sgc=True)
                      P.mm(obk[:, c0:512], Vp[par][th][:, sbk, :], a_[:, c0:512], start=False, stop=True, sgc=True)
                      if th == 1 and sbk == 0:
                          P.tt(DVE, OG[:, hp, qc * 512:(qc + 1) * 512], obk, sz[par][:, qc * 512:(qc + 1) * 512], ALU.mult)

                  nmicro = len(pending)
                  popc = [0, 0]

                  def pop():
                      popc[0] += 1
                      want = (popc[0] * nmicro) // (4 * n)
                      while popc[1] < want and pending:
                          pending.pop(0)()
                          popc[1] += 1
                  for th in range(2):
                      emit_z(th, 0)
                  S1(0, 0)
                  for i in range(n):
                      S1(1, i)
                      pop()
                      S2(0, i)
                      pop()
                      if i + 1 < n:
                          S1(0, i + 1)
                      pop()
                      S2(1, i)
                      pop()
                  while pending:
                      pending.pop(0)()

              if "B" not in SKIP:
                  load_w(0, 0)
                  for g in proj_groups(0, 0):
                      g()
              for hp in range(8 if "B" not in SKIP else 0):
                  par = hp % 2
                  pending = []
                  if hp + 1 < 8:
                      load_w(hp + 1, 1 - par)
                      pending = proj_groups(hp + 1, 1 - par)
                  if s == 0 and hp == 3:
                      late_consts()
                  attention_pair(hp, par, pending)
              if s == 0:
                  dbg_out("d_OG", OG[:])
              stop_at("B")

              if s > 0:
                  P.memset(POOL, Hin32[:], 0.0)
                  P.memset(POOL, Hin16[:], 0.0)
                  P.memset(POOL, carry[:], 0.0)
              ar = Arena()
              wpc = [ar.alloc((4, 8, 128), BF16) for _ in range(3)]
              wcnt = [0]
              YN = ar.alloc((16, 512), BF16)
              xs_tok0 = ar.alloc((4, 512), BF16)
              B_tok0 = ar.alloc((4, 128), BF16)
              szs0 = ar.alloc((4, 512), BF16)
              BT0 = ar.alloc((512,), BF16)
              CT0 = ar.alloc((512,), BF16)
              dts = []
              for _ in range(2):
                  dts.append(dict(dtv=ar.alloc((4, 32), F32), dA=ar.alloc((4, 32), F32), dA16=ar.alloc((4, 32), BF16),
                                  exps=ar.alloc((4, 96), F32), w1=ar.alloc((4, 32), F32), t32=ar.alloc((4, 32), F32)))
              mark = ar.off
              raw = [ar.alloc((516,), BF16) for _ in range(2)]
              dg = [ar.alloc((4, 128), BF16) for _ in range(2)]
              xsT = ar.alloc((4, 512), BF16)
              BT = [BT0, ar.alloc((512,), BF16)]
              CT = [CT0, ar.alloc((512,), BF16)]
              xs_tok = [xs_tok0, ar.alloc((4, 512), BF16)]
              B_tok = [B_tok0, ar.alloc((4, 128), BF16)]
              szs = [szs0, ar.alloc((4, 512), BF16)]
              rseg = ar.alloc((8, 128), BF16)
              E32 = ar.alloc((1024,), F32)
              Mb = ar.alloc((8, 128), BF16)
              CBm = ar.alloc((128,), F32)
              xdt = ar.alloc((512,), BF16)
              xw = ar.alloc((512,), BF16)
              xsD = ar.alloc((512,), BF16)
              y1 = ar.alloc((512,), F32)
              y2 = ar.alloc((512,), F32)
              y5 = ar.alloc((512,), BF16)
              jk = y5
              cmax = ar.off
              ar.off = mark
              MG = ar.alloc((8, 512), BF16)
              wd_ao = [ar.alloc((8, 128), BF16) for _ in range(2)]
              wd_so = [ar.alloc((16, 128), BF16) for _ in range(2)]
              wd_g = [ar.alloc((2, 8, 128), BF16) for _ in range(2)]
              wo_t = [ar.alloc((8, 512), BF16) for _ in range(2)]
              sg = [ar.alloc((512,), F32) for _ in range(2)]
              m1 = ar.alloc((512,), F32)
              m2 = ar.alloc((512,), F32)
              rr = ar.alloc((DM,), F32)
              fnw_bc = ar.alloc((DM,), F32)
              xt = [ar.alloc((DM,), F32) for _ in range(2)]
              ar.off = max(ar.off, cmax)
              pseg = bank(0, 2)
              pyd, pyo, pst, psm, ptr = bank(2), bank(0), bank(4), bank(5), bank(6)
              ppjs = [bank(7), bank(3)]
              ppjc = [0]
              ptr16 = ptr.bitcast(BF16)

              def load_piece(c0, n):
                  w = wpc[wcnt[0] % 3]
                  wcnt[0] += 1
                  P.dma(SP, w[:, 0:n], ws_in[c0:c0 + n].rearrange("c p k n -> p c k n"))
                  return w

              def c1_tasks(t0, g):
                  tasks = []
                  hold = {}

                  def gw(key, c0, n):
                      def f():
                          if key not in hold:
                              hold[key] = load_piece(c0, n)
                          return hold[key]
                      return f

                  def fm_conv(getw, wj, cc, dst, bi):
                      st_ = {}

                      def sa():
                          wt = getw()
                          st_["pb"] = ppjs[ppjc[0] % 2]
                          ppjc[0] += 1
                          for k in range(8):
                              P.mm(st_["pb"], wt[:, wj, k, :], hT[:, k, t0:t0 + 512], start=(k == 0), stop=(k == 7))

                      def sb_():
                          r_ = raw[bi]
                          P.copy(ACT, r_[:, 0:3], carry[:, cc, 0:3])
                          P.copy(DVE, r_[:, 3:515], st_["pb"])
                          P.copy(ACT, carry[:, cc, 0:3], r_[:, 512:515])
                          d_ = dg[bi]
                          for k in range(4):
                              P.act(d_[:, k, :], ident[:], AF.Identity, scale=convw[:, cc, k:k + 1])

                      def sc():
                          r_ = raw[bi]
                          d_ = dg[bi]
                          for k in range(4):
                              P.mm(st_["pb"], d_[:, k, :], r_[:, k:k + 512], start=(k == 0), stop=(k == 3))

                      def sd():
                          P.act(dst, st_["pb"], AF.Silu, bias=convb[:, cc:cc + 1])
                      return [sa, sb_, sc, sd]
                  lists = []
                  for j in range(4):
                      lists.append(fm_conv(gw("x", C_XS + g * 4, 4), j, g * 4 + j, xsT[:, j, :], j % 2))
                  lists.append(fm_conv(gw("b", C_B + g, 1), 0, 16 + g, BT[g % 2], 0))
                  lists.append(fm_conv(gw("c", C_C + g, 1), 0, 20 + g, CT[g % 2], 1))
                  for p0 in range(0, len(lists), 2):
                      for sidx in range(4):
                          for li in (p0, p0 + 1):
                              tasks.append(lists[li][sidx])
                  return tasks

              def dt_tasks(t0, D_):
                  tasks = []
                  hold = {}
                  lists = []

                  def chain(c):
                      st_ = {}
                      tk = slice(t0 + c * 128, t0 + (c + 1) * 128)
                      t32 = D_["t32"][:, c, :]

                      def s1():
                          if "w" not in hold:
                              hold["w"] = load_piece(C_DT, 1)
                          wdt = hold["w"]
                          st_["pb"] = ppjs[ppjc[0] % 2]
                          ppjc[0] += 1
                          for k in range(8):
                              P.mm(st_["pb"][:, 0:32], hT[:, k, tk], wdt[:, 0, k, 0:32], start=(k == 0), stop=(k == 7))

                      def s2():
                          P.tt(DVE, t32, st_["pb"][:, 0:32], dtb_bc[:], ALU.add)
                          P.act(t32, t32, AF.Exp)
                          P.act(D_["dtv"][:, c, :], t32, AF.Ln, bias=1.0)
                          P.tt(DVE, D_["dA"][:, c, :], D_["dtv"][:, c, :], a_bc[:], ALU.mult)
                          P.copy(DVE, D_["dA16"][:, c, :], D_["dA"][:, c, :])

                      def s3():
                          pb = st_["pb"]
                          P.mm(pb[:, 128:160], mLE[:], D_["dA16"][:, c, :])
                          P.mm(pb[:, 160:192], mTS[:], D_["dA16"][:, c, :])
                          P.mm(pb[:, 192:224], ones[:], D_["dA16"][:, c, :])

                      def s4():
                          P.act(D_["exps"][:, c, :], st_["pb"][:, 128:224], AF.Exp)
                          P.tt(DVE, D_["w1"][:, c, :], D_["exps"][:, c, 32:64], D_["dtv"][:, c, :], ALU.mult)
                      return [s1, s2, s3, s4]
                  for c in range(4):
                      lists.append(chain(c))
                  for p0 in range(0, 4, 2):
                      for sidx in range(4):
                          for li in (p0, p0 + 1):
                              tasks.append(lists[li][sidx])
                  return tasks

              for tt in range(4):
                  t0 = tt * 512
                  D_ = dts[tt % 2]
                  dtv, dA16, exps, w1 = D_["dtv"], D_["dA16"], D_["exps"], D_["w1"]

                  def c23_tasks(t0, g):
                      BTg = BT[g % 2]
                      xsk, Btk, szk = xs_tok[g % 2], B_tok[g % 2], szs[g % 2]
                      tasks = []
                      hold = {}

                      def c2a(c):
                          def t():
                              for j in range(4):
                                  P.tr(ptr16[:, j * 128:(j + 1) * 128], xsT[:, j, c * 128:(c + 1) * 128], ident[:])
                              P.tr(ptr16[:, 512:640], BTg[:, c * 128:(c + 1) * 128], ident[:])
                          return t

                      def c2b(c):
                          def t():
                              P.copy(DVE, xsk[:, c, :], ptr16[:, 0:512])
                              P.copy(DVE, Btk[:, c, :], ptr16[:, 512:640])
                          return t
                      st_ = {}

                      def c3a(c):
                          def t():
                              if "w" not in hold:
                                  hold["w"] = load_piece(C_ZS + g * 4, 4)
                              wz = hold["w"]
                              st_[c] = ppjs[ppjc[0] % 2]
                              ppjc[0] += 1
                              tk = slice(t0 + c * 128, t0 + (c + 1) * 128)
                              for k in range(8):
                                  P.mm(st_[c], hT[:, k, tk], wz[:, :, k, :], start=(k == 0), stop=(k == 7))
                          return t

                      def c3b(c):
                          def t():
                              P.act(szk[:, c, :], st_[c], AF.Silu)
                          return t
                      for c in range(4):
                          tasks.append(c3a(c))
                          if c > 0:
                              tasks.append(c3b(c - 1))
                      tasks.append(c3b(3))
                      for c in range(4):
                          tasks.append(c2a(c))
                          tasks.append(c2b(c))
                      return tasks

                  if tt == 0:
                      pend = dt_tasks(t0, D_) + c1_tasks(t0, 0) + c23_tasks(t0, 0)
                      while pend:
                          pend.pop(0)()
                      if s == 0:
                          dbg_out("d_dt", dtv[:, 0, :])
                          dbg_out("d_exps", exps[:, 0, :])
                  for g in range(4):
                      BTg, CTg = BT[g % 2], CT[g % 2]
                      xsk, Btk, szk = xs_tok[g % 2], B_tok[g % 2], szs[g % 2]
                      if s == 0 and tt == 0 and g == 0:
                          dbg_out("d_xsT", xsT)
                          dbg_out("d_BT", BTg)
                          dbg_out("d_xstok", xsk)
                          dbg_out("d_szs", szk)
                      if "noc23" in SKIP:
                          if g > 0:
                              for t_ in c23_tasks(t0, g):
                                  t_()
                          pend = c1_tasks(t0, g + 1) if g + 1 < 4 else []
                      elif g + 1 < 4:
                          pend = c1_tasks(t0, g + 1) + c23_tasks(t0, g + 1)
                      elif tt + 1 < 4:
                          pend = dt_tasks(t0 + 512, dts[(tt + 1) % 2]) + c1_tasks(t0 + 512, 0) + c23_tasks(t0 + 512, 0)
                      else:
                          pend = []
                      hs = slice(g * 8, (g + 1) * 8)
                      nslots = 22
                      per_slot = (len(pend) + nslots - 1) // nslots

                      def slot():
                          for _ in range(per_slot):
                              if pend:
                                  pend.pop(0)()
                      def cs_(c):
                          return slice(c * 128, (c + 1) * 128)

                      def xv_(c):
                          return xsk[:, c, :].rearrange("p (h d) -> p h d", h=8)

                      def pre_a(c):
                          P.mm(psm[:, 256:384], BTg[:, cs_(c)], CTg[:, cs_(c)])
                          P.tt(POOL, rseg, mLE[:].unsqueeze(1).broadcast_to([128, 8, 128]),
                               dA16[:, c, hs].unsqueeze(2).broadcast_to([128, 8, 128]), ALU.mult)
                          P.tt(DVE, CBm, psm[:, 256:384], mLE[:], ALU.mult)
                          for hf in range(2):
                              P.mm(pseg[:, hf * 512:(hf + 1) * 512], mTS[:], rseg[:, hf * 4:(hf + 1) * 4, :])
                          P.act(E32, pseg, AF.Exp)
                          P.tt(POOL, xdt.rearrange("p (h d) -> p h d", h=8), xv_(c),
                               dtv[:, c, hs].unsqueeze(2).broadcast_to([128, 8, 64]), ALU.mult)
                          P.tt(POOL, xsD.rearrange("p (h d) -> p h d", h=8), xv_(c),
                               dsk_bc[:, hs].unsqueeze(2).broadcast_to([128, 8, 64]), ALU.mult)
                          P.tt(POOL, xw.rearrange("p (h d) -> p h d", h=8), xv_(c),
                               w1[:, c, hs].unsqueeze(2).broadcast_to([128, 8, 64]), ALU.mult)

                      def pre_b(c):
                          P.tt(DVE, Mb, E32.rearrange("p (h l) -> p h l", h=8),
                               CBm.unsqueeze(1).broadcast_to([128, 8, 128]), ALU.mult)

                      def mid(c):
                          P.mm(pyd, ident[:], xsD, start=True, stop=True)
                          for h in range(8):
                              P.mm(pyd[:, h * 64:(h + 1) * 64], Mb[:, h, :], xdt[:, h * 64:(h + 1) * 64],
                                   start=False, stop=True, sgc=True)
                          P.mm(pyo, CTg[:, cs_(c)], Hin16[:, g, :])
                          P.mm(pst, Btk[:, c, :], xw)
                          P.tt(DVE, y1.rearrange("p (h d) -> p h d", h=8), pyo.rearrange("p (h d) -> p h d", h=8),
                               exps[:, c, g * 8:(g + 1) * 8].unsqueeze(2).broadcast_to([128, 8, 64]), ALU.mult)

                      def post_a(c):
                          P.tt(DVE, y2, y1, pyd, ALU.add)
                          P.tt(DVE, y1, y2, szk[:, c, :], ALU.mult)
                          P.act(jk, y1, AF.Square, accum_out=small[:, 4:5])
                          P.ts(DVE, small[:, 5:6], small[:, 4:5], 1.0 / 512, EPS, ALU.mult, ALU.add)
                          P.tt(POOL, small[:, 6:7], small[:, 5:6], nhalf[:], ALU.pow)

                      def post_b(c):
                          hv = Hin32[:, g, :].rearrange("p (h d) -> p h d", h=8)
                          P.tt(DVE, hv, hv, exps[:, c, 64 + g * 8:64 + (g + 1) * 8].unsqueeze(2).broadcast_to([128, 8, 64]),
                               ALU.mult)
                          P.tt(DVE, Hin32[:, g, :], Hin32[:, g, :], pst, ALU.add)
                          P.copy(ACT, Hin16[:, g, :], Hin32[:, g, :])
                          P.ts(DVE, y5, y1, small[:, 6:7], None, ALU.mult)
                          for j in range(4):
                              P.tr(ptr16[:, j * 128:(j + 1) * 128], y5[:, j * 128:(j + 1) * 128], ident[:])
                          for j in range(4):
                              P.act(YN[:, g * 4 + j, cs_(c)], ptr16[:, j * 128:(j + 1) * 128], AF.Identity,
                                    scale=snw_col[:, g * 4 + j:g * 4 + j + 1])

                      pre_a(0)
                      slot()
                      pre_b(0)
                      mid(0)
                      slot()
                      for c in range(4):
                          if c + 1 < 4:
                              pre_a(c + 1)
                          slot()
                          post_a(c)
                          slot()
                          if c + 1 < 4:
                              pre_b(c + 1)
                          slot()
                          post_b(c)
                          slot()
                          if c + 1 < 4:
                              mid(c + 1)
                          slot()
                      while pend:
                          pend.pop(0)()
                  if s == 0 and tt == 0:
                      dbg_out("d_YN", YN)
                  stop_at("C")

                  def load_d(dc):
                      i = dc % 2
                      P.dma(SP, wd_ao[i], ws_ao[dc])
                      P.dma(SP, wd_so[i], ws_so[dc])
                      P.dma(SP, wd_g[i][:, 0], ws_in[C_GA + dc])
                      P.dma(SP, wd_g[i][:, 1], ws_in[C_GS + dc])
                  P.dma(SP, fnw_bc, final_norm_w.broadcast_to([128, DM]))
                  load_d(0)
                  for dc in range(8):
                      i = dc % 2
                      if dc + 1 < 8:
                          load_d(dc + 1)
                      else:
                          for nh in range(2):
                              P.dma(SP, wo_t[nh], ws_o[nh])
                      pA, pB, pC, pD = [bank(4 * i + q) for q in range(4)]
                      for k in range(8):
                          P.mm(pC, wd_g[i][:, 0, k, :], hT[:, k, t0:t0 + 512], start=(k == 0), stop=(k == 7))
                      for k in range(8):
                          P.mm(pD, wd_g[i][:, 1, k, :], hT[:, k, t0:t0 + 512], start=(k == 0), stop=(k == 7))
                      for k in range(8):
                          P.mm(pA, wd_ao[i][:, k, :], OG[:, k, t0:t0 + 512], start=(k == 0), stop=(k == 7))
                      for k in range(16):
                          P.mm(pB, wd_so[i][:, k, :], YN[:, k, :], start=(k == 0), stop=(k == 15))
                      P.act(sg[0], pC, AF.Sigmoid)
                      P.act(sg[1], pD, AF.Sigmoid)
                      P.tt(DVE, m1, sg[0], pA, ALU.mult)
                      P.tt(DVE, m2, sg[1], pB, ALU.mult)
                      P.tt(DVE, MG[:, dc, :], m1, m2, ALU.add)
                  if s == 0 and tt == 0:
                      dbg_out("d_MG", MG)
                  for tb in range(4):
                      r0 = row0 + t0 + tb * 128
                      xi = xt[tb % 2]
                      P.dma(SP, xi, x[r0:r0 + 128, :])
                      for nh in range(2):
                          po = bank(2 * (tb % 2) + nh)
                          for k in range(8):
                              P.mm(po, MG[:, k, tb * 128:(tb + 1) * 128], wo_t[nh][:, k, :], start=(k == 0), stop=(k == 7))
                          P.tt(DVE, rr[:, nh * 512:(nh + 1) * 512], xi[:, nh * 512:(nh + 1) * 512], po, ALU.add)
                      P.act(xi, rr, AF.Square, accum_out=small[:, 8:9])
                      P.ts(DVE, small[:, 9:10], small[:, 8:9], 1.0 / DM, EPS, ALU.mult, ALU.add)
                      P.tt(POOL, small[:, 10:11], small[:, 9:10], nhalf[:], ALU.pow)
                      P.stt(xi, rr, small[:, 10:11], fnw_bc, ALU.mult, ALU.mult)
                      P.dma(SP, y[r0:r0 + 128, :], xi)
        try:
            main_body()
        except StopBuild:
            pass
        P.fence(SP, reads=[y])
        for nm in dbg:
            P.fence(SP, reads=[dbg[nm]])
        P.emit()
    return nc


_NC_CACHE = {}


def kernel(x, norm_w, w_in, conv_w, conv_b, dt_bias, a_log, d_skip, ssm_norm_w,
           w_attn_out, w_ssm_out, w_o, final_norm_w):
    ncores = 8
    if "nc" not in _NC_CACHE:
        _NC_CACHE["nc"] = build_program()
    nc = _NC_CACHE["nc"]
    f = lambda a: np.ascontiguousarray(np.asarray(a, dtype=np.float32))
    x = f(x)
    common = {
        "norm_w": f(norm_w).reshape(1, DM), "w_in": f(w_in).reshape(DM, 11296),
        "conv_w": f(conv_w).reshape(4, 3072), "conv_b": f(conv_b).reshape(1, 3072),
        "dt_bias": f(dt_bias).reshape(1, 32), "a_log": f(a_log).reshape(1, 32),
        "d_skip": f(d_skip).reshape(1, 32), "ssm_norm_w": f(ssm_norm_w).reshape(1, 2048),
        "w_attn_out": f(w_attn_out).reshape(1024, 1024), "w_ssm_out": f(w_ssm_out).reshape(2048, 1024),
        "w_o": f(w_o).reshape(1024, 1024), "final_norm_w": f(final_norm_w).reshape(1, DM),
    }
    in_maps = []
    for c in range(ncores):
        m = dict(common)
        m["x"] = x[c * NSEQ:(c + 1) * NSEQ].reshape(NSEQ * S, DM)
        in_maps.append(m)
    res = run_bass_kernel_spmd(nc, in_maps, core_ids=list(range(ncores)))
    out = np.concatenate([r["y"].reshape(NSEQ, S, DM) for r in res.results], axis=0)
    return out.astype(np.float32)
```
